# Optimizing a Trainium2 kernel written in Bass

```python
import math
import jax, jax.numpy as jnp
from jax import lax
import numpy as np

D_MODEL = 2048
BATCH = 4
SEQ = 2048
DEPTH = 4

GRID_W = 64
CTX_LEN = 256
N_MIXERS = 3
EPS = 1e-6

POOL_WIDTH = 2 * D_MODEL
POOL_WINDOWS = (2, 4, 8, 16)
POOL_GROUP = POOL_WIDTH // len(POOL_WINDOWS)
GMLP_WIDTH = 2 * D_MODEL
GMLP_CHUNK = 128
GMLP_GROUPS = 8
GMLP_GROUP_W = GMLP_WIDTH // GMLP_GROUPS
DIFF_HEAD_DIM = 128
DIFF_V_DIM = 2 * DIFF_HEAD_DIM
DIFF_HEADS = D_MODEL // DIFF_V_DIM
DIFF_WIDTH = DIFF_HEADS * DIFF_V_DIM
Q_BLOCK = 128
ROPE_AXIS_DIM = DIFF_HEAD_DIM // 2
ROPE_BASE = 10000.0

N_POOL = len(range(0, DEPTH, N_MIXERS))
N_GMLP = len(range(1, DEPTH, N_MIXERS))
N_DIFF = len(range(2, DEPTH, N_MIXERS))

kernel_name = "interleaved_pool_gmlp_diffattn_dit"


def rms_norm(x, g):
    xf = x.astype(jnp.float32)
    y = xf * lax.rsqrt(jnp.mean(xf * xf, axis=-1, keepdims=True) + EPS)
    return (y * g.astype(jnp.float32)).astype(x.dtype)


def layer_norm(x, g, b):
    xf = x.astype(jnp.float32)
    mu = jnp.mean(xf, axis=-1, keepdims=True)
    var = jnp.mean(jnp.square(xf - mu), axis=-1, keepdims=True)
    y = (xf - mu) * lax.rsqrt(var + EPS)
    return (y * g.astype(jnp.float32) + b.astype(jnp.float32)).astype(x.dtype)


def axial_rope_tables(L):
    rows = L // GRID_W
    row = jnp.repeat(jnp.arange(rows), GRID_W).astype(jnp.float32)
    col = jnp.tile(jnp.arange(GRID_W), rows).astype(jnp.float32)
    inv = ROPE_BASE ** (-jnp.arange(0, ROPE_AXIS_DIM, 2, dtype=jnp.float32) / ROPE_AXIS_DIM)
    ang_r = row[:, None] * inv[None, :]
    ang_c = col[:, None] * inv[None, :]
    return (jnp.cos(ang_r), jnp.sin(ang_r), jnp.cos(ang_c), jnp.sin(ang_c))


def _rotate(xa, cos, sin):
    half = xa.shape[-1] // 2
    x1, x2 = xa[..., :half], xa[..., half:]
    return jnp.concatenate([x1 * cos - x2 * sin, x1 * sin + x2 * cos], axis=-1)


def apply_axial_rope(x, rope):
    cos_r, sin_r, cos_c, sin_c = [r[None, :, None, None, :] for r in rope]
    xf = x.astype(jnp.float32)
    xr = _rotate(xf[..., :ROPE_AXIS_DIM], cos_r, sin_r)
    xc = _rotate(xf[..., ROPE_AXIS_DIM:], cos_c, sin_c)
    return jnp.concatenate([xr, xc], axis=-1).astype(x.dtype)


def pool_mix(z, w_grp):
    B_, L, W = z.shape
    zf = z.astype(jnp.float32)
    cs = jnp.concatenate([jnp.zeros((B_, 1, W), jnp.float32), jnp.cumsum(zf, axis=1)], axis=1)
    t = jnp.arange(L)
    outs = []
    for gi, w in enumerate(POOL_WINDOWS):
        lo = jnp.clip(t - w // 2, 0, L)
        hi = jnp.clip(t + w - w // 2, 0, L)
        sl = slice(gi * POOL_GROUP, (gi + 1) * POOL_GROUP)
        csg = cs[..., sl]
        cnt = (hi - lo).astype(jnp.float32)[None, :, None]
        mean = (jnp.take(csg, hi, axis=1) - jnp.take(csg, lo, axis=1)) / cnt
        outs.append(mean - zf[..., sl])
    p = jnp.stack(outs, axis=2).astype(z.dtype)
    y = jnp.einsum('blgi,gio->blgo', p, w_grp)
    return y.reshape(B_, L, W)


def pool_branch(h, w_in, w_grp, scale, w_out):
    z, g = jnp.split(h @ w_in, 2, axis=-1)
    y = pool_mix(z, w_grp) * scale
    return (y * jax.nn.silu(g)) @ w_out


def gmlp_branch(h, w_in, ln_g, ln_b, w_s, b_s, w_out):
    B_, L, _ = h.shape
    z = h @ w_in
    uv, g = z[..., :2 * GMLP_WIDTH], z[..., 2 * GMLP_WIDTH:]
    u, v = jnp.split(jax.nn.gelu(uv), 2, axis=-1)
    v = layer_norm(v, ln_g, ln_b)
    n = L // GMLP_CHUNK
    vc = v.reshape(B_, n, GMLP_CHUNK, GMLP_GROUPS, GMLP_GROUP_W)
    s = jnp.einsum('hpq,bnqhc->bnphc', w_s, vc) + b_s.T[None, None, :, :, None]
    s = s.reshape(B_, L, GMLP_WIDTH)
    return (u * s * jax.nn.silu(g)) @ w_out


def diff_branch(hx, hc, w_in, qn_g, kn_g, lq1, lk1, lq2, lk2, subln_g, w_out, lam_init, rope, need_ctx):
    scale = DIFF_HEAD_DIM ** -0.5

    def project(h, use_rope):
        B_, L_, _ = h.shape
        q, k, v, g = jnp.split(h @ w_in, 4, axis=-1)
        q = rms_norm(q.reshape(B_, L_, DIFF_HEADS, 2, DIFF_HEAD_DIM), qn_g)
        k = rms_norm(k.reshape(B_, L_, DIFF_HEADS, 2, DIFF_HEAD_DIM), kn_g)
        if use_rope:
            q = apply_axial_rope(q, rope)
            k = apply_axial_rope(k, rope)
        return q, k, v.reshape(B_, L_, DIFF_HEADS, DIFF_V_DIM), g

    lam = (jnp.exp(jnp.sum(lq1.astype(jnp.float32) * lk1.astype(jnp.float32)))
           - jnp.exp(jnp.sum(lq2.astype(jnp.float32) * lk2.astype(jnp.float32))) + lam_init)

    def attend(q, k, v):
        s = jnp.einsum('bqhid,bkhid->bhiqk', q, k).astype(jnp.float32) * scale
        p = jax.nn.softmax(s, axis=-1)
        a = (p[:, :, 0] - lam * p[:, :, 1]).astype(v.dtype)
        return jnp.einsum('bhqk,bkhe->bqhe', a, v)

    qx, kx, vx, gx = project(hx, True)
    qc, kc, vc, gc = project(hc, False)
    k_all = jnp.concatenate([kc, kx], axis=1)
    v_all = jnp.concatenate([vc, vx], axis=1)
    B_, L, _ = hx.shape
    qb = jnp.swapaxes(qx.reshape(B_, L // Q_BLOCK, Q_BLOCK, DIFF_HEADS, 2, DIFF_HEAD_DIM), 0, 1)
    ox = lax.map(lambda q_blk: attend(q_blk, k_all, v_all), qb)
    ox = jnp.swapaxes(ox, 0, 1).reshape(B_, L, DIFF_HEADS, DIFF_V_DIM)

    def finish(o, g):
        o = rms_norm(o, subln_g) * (1.0 - lam_init)
        return (o.reshape(o.shape[0], o.shape[1], DIFF_WIDTH) * jax.nn.silu(g)) @ w_out

    yx = finish(ox, gx)
    yc = finish(attend(qc, kc, vc), gc) if need_ctx else None
    return yx, yc


def setup_inputs(seed: int = 0) -> dict:
    key = jax.random.key(seed)
    ks = jax.random.split(key, 32)
    f32 = jnp.float32

    def nrm(k, shape, s):
        return jax.random.normal(k, shape, f32) * s

    return {
        "x": nrm(ks[0], (BATCH, SEQ, D_MODEL), 1.0),
        "c": nrm(ks[1], (BATCH, D_MODEL), 1.0),
        "ctx": nrm(ks[2], (BATCH, CTX_LEN, D_MODEL), 1.0),
        "c_ctx": nrm(ks[3], (D_MODEL,), 1.0),
        "norm_g": 1.0 + nrm(ks[4], (DEPTH, D_MODEL), 0.02),
        "ada_w": nrm(ks[5], (DEPTH, D_MODEL, 3 * D_MODEL), 0.5 * D_MODEL ** -0.5),
        "ada_b": nrm(ks[6], (DEPTH, 3 * D_MODEL), 0.02),
        "pool_w_in": nrm(ks[7], (N_POOL, D_MODEL, 2 * POOL_WIDTH), D_MODEL ** -0.5),
        "pool_w_grp": nrm(ks[8], (N_POOL, len(POOL_WINDOWS), POOL_GROUP, POOL_GROUP), POOL_GROUP ** -0.5),
        "pool_scale": 1.0 + nrm(ks[9], (N_POOL, POOL_WIDTH), 0.1),
        "pool_w_out": nrm(ks[10], (N_POOL, POOL_WIDTH, D_MODEL), POOL_WIDTH ** -0.5),
        "gmlp_w_in": nrm(ks[11], (N_GMLP, D_MODEL, 3 * GMLP_WIDTH), D_MODEL ** -0.5),
        "gmlp_ln_g": 1.0 + nrm(ks[12], (N_GMLP, GMLP_WIDTH), 0.02),
        "gmlp_ln_b": nrm(ks[13], (N_GMLP, GMLP_WIDTH), 0.02),
        "gmlp_w_s": nrm(ks[14], (N_GMLP, GMLP_GROUPS, GMLP_CHUNK, GMLP_CHUNK), GMLP_CHUNK ** -0.5),
        "gmlp_b_s": 1.0 + nrm(ks[15], (N_GMLP, GMLP_GROUPS, GMLP_CHUNK), 0.02),
        "gmlp_w_out": nrm(ks[16], (N_GMLP, GMLP_WIDTH, D_MODEL), GMLP_WIDTH ** -0.5),
        "diff_w_in": nrm(ks[17], (N_DIFF, D_MODEL, 4 * DIFF_WIDTH), D_MODEL ** -0.5),
        "diff_q_norm_g": 1.0 + nrm(ks[18], (N_DIFF, DIFF_HEAD_DIM), 0.02),
        "diff_k_norm_g": 1.0 + nrm(ks[19], (N_DIFF, DIFF_HEAD_DIM), 0.02),
        "diff_lq1": nrm(ks[20], (N_DIFF, DIFF_HEAD_DIM), 0.1),
        "diff_lk1": nrm(ks[21], (N_DIFF, DIFF_HEAD_DIM), 0.1),
        "diff_lq2": nrm(ks[22], (N_DIFF, DIFF_HEAD_DIM), 0.1),
        "diff_lk2": nrm(ks[23], (N_DIFF, DIFF_HEAD_DIM), 0.1),
        "diff_subln_g": 1.0 + nrm(ks[24], (N_DIFF, DIFF_V_DIM), 0.02),
        "diff_w_out": nrm(ks[25], (N_DIFF, DIFF_WIDTH, D_MODEL), DIFF_WIDTH ** -0.5),
    }


def reference(x, c, ctx, c_ctx, norm_g, ada_w, ada_b,
              pool_w_in, pool_w_grp, pool_scale, pool_w_out,
              gmlp_w_in, gmlp_ln_g, gmlp_ln_b, gmlp_w_s, gmlp_b_s, gmlp_w_out,
              diff_w_in, diff_q_norm_g, diff_k_norm_g, diff_lq1, diff_lk1, diff_lq2, diff_lk2,
              diff_subln_g, diff_w_out):
    L = x.shape[1]
    rope = axial_rope_tables(L)
    sc = jax.nn.silu(c)[:, None, :]
    scc = jax.nn.silu(c_ctx)[None, None, :]
    for i in range(DEPTH):
        kind, j = i % N_MIXERS, i // N_MIXERS
        need_ctx = i < DEPTH - 1
        shift_x, scale_x, gate_x = jnp.split(sc @ ada_w[i] + ada_b[i], 3, axis=-1)
        hx = rms_norm(x, norm_g[i]) * (1.0 + scale_x) + shift_x
        if need_ctx or kind == 2:
            shift_c, scale_c, gate_c = jnp.split(scc @ ada_w[i] + ada_b[i], 3, axis=-1)
            hc = rms_norm(ctx, norm_g[i]) * (1.0 + scale_c) + shift_c
        if kind == 0:
            args = (pool_w_in[j], pool_w_grp[j], pool_scale[j], pool_w_out[j])
            yx = pool_branch(hx, *args)
            yc = pool_branch(hc, *args) if need_ctx else None
        elif kind == 1:
            args = (gmlp_w_in[j], gmlp_ln_g[j], gmlp_ln_b[j], gmlp_w_s[j], gmlp_b_s[j], gmlp_w_out[j])
            yx = gmlp_branch(hx, *args)
            yc = gmlp_branch(hc, *args) if need_ctx else None
        else:
            lam_init = 0.8 - 0.6 * math.exp(-0.3 * i)
            yx, yc = diff_branch(hx, hc, diff_w_in[j], diff_q_norm_g[j], diff_k_norm_g[j],
                                 diff_lq1[j], diff_lk1[j], diff_lq2[j], diff_lk2[j],
                                 diff_subln_g[j], diff_w_out[j], lam_init, rope, need_ctx)
        x = x + gate_x * yx
        if need_ctx:
            ctx = ctx + gate_c * yc
    return x
```

```python
import math
from contextlib import ExitStack

import numpy as np
import concourse.bass as bass
import concourse.mybir as mybir
from concourse.bass_utils import run_bass_kernel_spmd

F32 = mybir.dt.float32
BF16 = mybir.dt.bfloat16
AF = mybir.ActivationFunctionType
ALU = mybir.AluOpType

D = 2048
KC = 16
NT = 1184
NIN = 1152
EPS = 1e-6
ENGINES = ["tensor", "vector", "scalar", "gpsimd", "sync"]
PIECES = [(0, 512), (512, 1024), (1024, 1184)]
INNER = [(0, 512), (512, 1024), (1024, 1152)]
WSLOT = 4096
NSLOT = 3
import os as _os_dbg
NO_ADA_BG = bool(_os_dbg.environ.get("NO_ADA_BG"))


def piece_ids(c0, c1):
    return [i for i, (a, b) in enumerate(PIECES) if a < c1 and c0 < b]


class Prog:
    def __init__(self, nc, ctx):
        self.nc = nc
        self.ctx = ctx
        self.q = {e: [] for e in ENGINES}
        self.sems = {}
        self.cnt = {}
        for e in ENGINES:
            self._mksem("e_" + e)
        self.waited = {e: {} for e in ENGINES}
        self.res = {}

    def _mksem(self, name):
        if name not in self.sems:
            self.sems[name] = self.ctx.enter_context(self.nc.semaphore(name))
            self.cnt[name] = 0
        return self.sems[name]

    def _deps(self, reads, writes):
        d = {}

        def add(s, v):
            if d.get(s, 0) < v:
                d[s] = v

        for k in reads:
            r = self.res.get(k)
            if r is not None and r[0] is not None:
                add(*r[0])
        for k in writes:
            r = self.res.get(k)
            if r is not None:
                if r[0] is not None:
                    add(*r[0])
                for s, v in r[1].items():
                    add(s, v)
        return d

    def _commit(self, ev, reads, writes):
        s, v = ev
        for k in reads:
            r = self.res.get(k)
            if r is None:
                r = self.res[k] = [None, {}]
            if r[1].get(s, 0) < v:
                r[1][s] = v
        for k in writes:
            self.res[k] = [ev, {}]

    def _waits(self, eng, deps):
        w = []
        wd = self.waited[eng]
        for s, v in deps.items():
            if wd.get(s, 0) < v:
                wd[s] = v
                w.append((self.sems[s], v))
        return w

    def op(self, eng, fn, reads=(), writes=()):
        deps = self._deps(reads, writes)
        waits = self._waits(eng, deps)
        sname = "e_" + eng
        self.cnt[sname] += 1
        ev = (sname, self.cnt[sname])
        sem = self.sems[sname]

        def emit(e, waits=waits, fn=fn, sem=sem):
            for s, v in waits:
                e.wait_ge(s, v)
            fn(e).then_inc(sem, 1)

        self.q[eng].append(emit)
        self._commit(ev, reads, writes)
        return ev

    def dma(self, queue, dsem, out, in_, reads=(), writes=()):
        deps = self._deps(reads, writes)
        waits = self._waits(queue, deps)
        sem = self._mksem(dsem)
        self.cnt[dsem] += 16
        ev = (dsem, self.cnt[dsem])

        def emit(e, waits=waits, sem=sem, out=out, in_=in_):
            for s, v in waits:
                e.wait_ge(s, v)
            e.dma_start(out=out, in_=in_).then_inc(sem, 16)

        self.q[queue].append(emit)
        self._commit(ev, reads, writes)
        return ev

    def wait_all(self, eng, keys):
        deps = self._deps((), keys)
        waits = self._waits(eng, deps)

        def emit(e, waits=waits):
            for s, v in waits:
                e.wait_ge(s, v)

        self.q[eng].append(emit)

    def barrier(self, engines=("tensor", "vector", "scalar", "sync")):
        snap = {s: v for s, v in self.cnt.items() if v > 0}
        for eng in engines:
            waits = self._waits(eng, dict(snap))

            def emit(e, waits=waits):
                for s, v in waits:
                    e.wait_ge(s, v)

            self.q[eng].append(emit)

    def emit_all(self):
        with self.nc.Block() as block:
            @block.tensor
            def _(e):
                for f in self.q["tensor"]:
                    f(e)

            @block.vector
            def _(e):
                for f in self.q["vector"]:
                    f(e)

            @block.scalar
            def _(e):
                for f in self.q["scalar"]:
                    f(e)

            @block.gpsimd
            def _(e):
                for f in self.q["gpsimd"]:
                    f(e)

            @block.sync
            def _(e):
                for f in self.q["sync"]:
                    f(e)


class Builder:
    def __init__(self, mode, dump=False, lite=False, stop=None):
        self.lite = lite
        self.stop = stop
        self.mode = mode
        self.dump = dump
        self.nc = bass.Bass("TRN2", target_bir_lowering=False)
        self.dram = {}
        self.ps_rr = 0
        self.st_rr = 0
        self.w_i = 0
        self.ada_done = set()
        self.ada_queue = []
        self.ada_inflight = None
        self.ada_cur = None
        self.okeys = []

    def dump_pt(self, nm):
        if nm in self.dump_t:
            self.P.dma("sync", "d_" + nm, self.dump_t[nm][:, :, :], self.xT[:],
                       reads=[("x", k, p) for k in range(KC) for p in range(3)], writes=[self.okey()])

    def okey(self):
        k = ("okey", len(self.okeys))
        self.okeys.append(k)
        return k

    def din(self, name, shape, dt=F32):
        t = self.nc.dram_tensor(name, list(shape), dt, kind="ExternalInput").ap()
        self.dram[name] = t
        return t

    def dout(self, name, shape, dt=F32):
        t = self.nc.dram_tensor(name, list(shape), dt, kind="ExternalOutput").ap()
        self.dram[name] = t
        return t

    def dint(self, name, shape, dt=F32):
        t = self.nc.dram_tensor(name, list(shape), dt, kind="Internal").ap()
        self.dram[name] = t
        return t

    def act(self, out, in_, func, reads, writes, **kw):
        return self.P.op("scalar", lambda e: e.activation(out=out, in_=in_, func=func, **kw), reads, writes)

    def tt(self, out, in0, in1, op, reads, writes):
        return self.P.op("vector", lambda e: e.tensor_tensor(out=out, in0=in0, in1=in1, op=op), reads, writes)

    def ts(self, out, in0, s1, s2, op0, op1, reads, writes):
        if s2 is None:
            return self.P.op("vector", lambda e: e.tensor_scalar(out=out, in0=in0, scalar1=s1, scalar2=None, op0=op0),
                             reads, writes)
        return self.P.op("vector", lambda e: e.tensor_scalar(out=out, in0=in0, scalar1=s1, scalar2=s2, op0=op0, op1=op1),
                         reads, writes)

    def stt(self, out, in0, scalar, in1, op0, op1, reads, writes):
        return self.P.op("vector", lambda e: e.scalar_tensor_tensor(out=out, in0=in0, scalar=scalar, in1=in1, op0=op0, op1=op1),
                         reads, writes)

    def vcopy(self, out, in_, reads, writes):
        return self.P.op("vector", lambda e: e.tensor_copy(out=out, in_=in_), reads, writes)

    def recip(self, out, in_, reads, writes):
        return self.P.op("vector", lambda e: e.reciprocal(out=out, in_=in_), reads, writes)

    def pe(self, mms, reads, writes):
        def fn(e, mms=mms):
            ins = None
            for (o, l, r, st, sp) in mms:
                ins = e.matmul(o, l, r, start=st, stop=sp)
            return ins
        return self.P.op("tensor", fn, reads, writes)

    def pet(self, tps, reads, writes):
        def fn(e, tps=tps):
            ins = None
            for (o, i, idn) in tps:
                ins = e.transpose(o, i, idn)
            return ins
        return self.P.op("tensor", fn, reads, writes)

    def bank(self, b, dt=F32):
        a = self.ps[:, b * 512:(b + 1) * 512]
        return a if dt == F32 else a.bitcast(dt)

    def wload(self, w2d, kc, c0, ncols):
        r = self.wload_raw(w2d, kc, c0, ncols)
        self.ada_pump()
        return r

    def wload_raw(self, w2d, kc, c0, ncols):
        s = self.w_i % NSLOT
        self.w_i += 1
        assert kc * ncols <= WSLOT
        view = self.wslots[:, s, 0:kc * ncols].rearrange("p (k n) -> p k n", n=ncols)
        src = w2d.rearrange("(k p) n -> p k n", p=128)[:, :, c0:c0 + ncols]
        self.P.dma("gpsimd", "w%d" % s, view, src, writes=[("w", s)])
        return view, ("w", s)

    def arena_reset(self):
        self.a_off = 0

    def carve(self, shape, dt):
        n = 1
        for d_ in shape[1:]:
            n *= d_
        words = n if dt == F32 else (n + 1) // 2
        words = (words + 15) // 16 * 16
        o = self.a_off
        self.a_off += words
        assert self.a_off <= self.ARENA, (self.a_off, self.ARENA)
        v = self.arena[:, o:o + words]
        if dt != F32:
            v = v.bitcast(dt)
        v = v[:, 0:n]
        if len(shape) == 2:
            return v
        if len(shape) == 3:
            return v.rearrange("p (a b) -> p a b", b=shape[2])
        return v.rearrange("p (a b c) -> p a b c", b=shape[2], c=shape[3])

    def build(self):
        nc = self.nc
        mode = self.mode
        fused = mode == "fused"
        if mode in ("fused",):
            xa = self.din("xa", [128, KC, NT])
            hmask_a = self.din("hmask_a", [128, 32])
            ecorr_a = self.din("ecorr_a", [128, 128])
            rope_a = self.din("rope_a", [128, 2, NT])
        xb = self.din("xb", [128, KC, NT])
        hmask_b = self.din("hmask_b", [128, 32])
        ecorr_b = self.din("ecorr_b", [128, 128])
        rope_b = self.din("rope_b", [128, 2, NT])
        cvec = self.din("cvec", [128, KC, 2])
        ident = self.din("ident", [128, 128])
        permT = self.din("permT", [128, 128])
        norm_gT = self.din("norm_gT", [128, 4, KC])
        ada_bT = self.din("ada_bT", [128, 4, 48, 2])
        if self.lite:
            ada_w = self.din("ada_w", [4, 128, 128])
            pool_w_in = self.din("pool_w_in", [2, 128, 128])
            pool_w_grp = self.din("pool_w_grp", [2, 4, 128, 128])
            pool_w_out = self.din("pool_w_out", [2, 128, 128])
            gmlp_w_in = self.din("gmlp_w_in", [128, 128])
        else:
            ada_w = self.din("ada_w", [4, D, 3 * D])
            pool_w_in = self.din("pool_w_in", [2, D, 8192])
            pool_w_grp = self.din("pool_w_grp", [2, 4, 1024, 1024])
            pool_w_out = self.din("pool_w_out", [2, 4096, D])
            gmlp_w_in = self.din("gmlp_w_in", [D, 12288])
        pool_scaleT = self.din("pool_scaleT", [128, 2, 32])
        ln_gbT = self.din("ln_gbT", [128, 2, 32])
        w_sT = self.din("w_sT", [128, 8, 128])
        b_s_row = self.din("b_s_row", [1, 1024])
        gmlp_w_out = self.din("gmlp_w_out", [128, 128] if self.lite else [4096, D])
        diff_w_in = self.din("diff_w_in", [D, 8192])
        dvec = self.din("dvec", [128, 8])
        diff_w_out = self.din("diff_w_out", [D, D])
        out = self.dout("out", [128, KC, 1024])
        self.dump_t = {}
        if self.dump:
            xdump = self.dout("xdump", [128, KC, NT])
            for nm in ("dA1", "dB0", "dB1", "dB2"):
                self.dump_t[nm] = self.dout(nm, [128, KC, NT])
        if mode == "u1":
            k_oth = self.dout("k_oth", [D, NIN], BF16)
            v_oth = self.dout("v_oth", [NIN, D], BF16)
        elif mode == "u2":
            k_oth = self.din("k_oth", [D, NIN], BF16)
            v_oth = self.din("v_oth", [NIN, D], BF16)
        else:
            k_oth = self.dint("k_oth", [D, NIN], BF16)
            v_oth = self.dint("v_oth", [NIN, D], BF16)
        k_own = self.dint("k_own", [D, NIN], BF16)
        v_own = self.dint("v_own", [NIN, D], BF16)
        q_own = self.dint("q_own", [D, NT], BF16)
        sg_own = self.dint("sg_own", [D, NT], BF16)

        with ExitStack() as ctx:
            E = ctx.enter_context
            self.xT = E(nc.sbuf_tensor("xT", [128, KC, NT], F32))
            self.wslots = E(nc.sbuf_tensor("wslots", [128, NSLOT, WSLOT], BF16))
            self.ident_f = E(nc.sbuf_tensor("ident_f", [128, 128], F32))
            self.ident_b = E(nc.sbuf_tensor("ident_b", [128, 128], BF16))
            self.ones_b = E(nc.sbuf_tensor("ones_b", [128, 128], BF16))
            self.ones_f = E(nc.sbuf_tensor("ones_f", [128, 128], F32))
            self.perm_f = E(nc.sbuf_tensor("perm_f", [128, 128], F32))
            self.perm_b = E(nc.sbuf_tensor("perm_b", [128, 128], BF16))
            self.cv = E(nc.sbuf_tensor("cv", [128, KC, 2], F32))
            self.scv = E(nc.sbuf_tensor("scv", [128, KC, 2], BF16))
            self.modall = E(nc.sbuf_tensor("modall", [128, 4, 48, 2], F32))
            self.modA = E(nc.sbuf_tensor("modA", [128, 4, KC, 2], F32))
            self.ngT = E(nc.sbuf_tensor("ngT", [128, 4, KC], F32))
            self.abT = E(nc.sbuf_tensor("abT", [128, 4, 48, 2], F32))
            self.pscT = E(nc.sbuf_tensor("pscT", [128, 2, 32], F32))
            self.lngb = E(nc.sbuf_tensor("lngb", [128, 2, 32], F32))
            self.dv = E(nc.sbuf_tensor("dv", [128, 8], F32))
            self.dsm = E(nc.sbuf_tensor("dsm", [128, 16], F32))
            self.adarow = E(nc.sbuf_tensor("adarow", [2, 256], F32))
            self.xsave = E(nc.sbuf_tensor("xsave", [128, KC, 16], F32))
            self.hmask = E(nc.sbuf_tensor("hmask", [128, 32], F32))
            self.ecorr = E(nc.sbuf_tensor("ecorr", [128, 128], F32))
            self.ARENA = 25472
            self.arena = E(nc.sbuf_tensor("arena", [128, self.ARENA], F32))
            self.ps = E(nc.psum_tensor("ps", [128, 4096], F32))
            self.P = P = Prog(nc, ctx)

            P.dma("sync", "d_c0", self.ident_f[:], ident[:, :], writes=["ident_f"])
            P.dma("sync", "d_c1", self.perm_f[:], permT[:, :], writes=["perm_f"])
            P.dma("sync", "d_c2", self.cv[:], cvec[:, :, :], writes=["cv"])
            P.dma("sync", "d_c3", self.ngT[:], norm_gT[:, :, :], writes=["ngT"])
            P.dma("sync", "d_c4", self.abT[:], ada_bT[:, :, :, :], writes=["abT"])
            P.dma("sync", "d_c5", self.pscT[:], pool_scaleT[:, :, :], writes=["pscT"])
            P.dma("sync", "d_c6", self.lngb[:], ln_gbT[:, :, :], writes=["lngb"])
            P.dma("sync", "d_c7", self.dv[:], dvec[:, :], writes=["dv"])
            P.op("vector", lambda e: e.memset(self.ones_b[:], 1.0), writes=["ones_b"])
            P.op("vector", lambda e: e.memset(self.ones_f[:], 1.0), writes=["ones_f"])
            self.vcopy(self.ident_b[:], self.ident_f[:], ["ident_f"], ["ident_b"])
            self.vcopy(self.perm_b[:], self.perm_f[:], ["perm_f"], ["perm_b"])
            self.act(self.scv[:], self.cv[:], AF.Silu, ["cv"], ["scv"])

            self.w = dict(ada_w=ada_w, pool_w_in=pool_w_in, pool_w_grp=pool_w_grp, pool_w_out=pool_w_out,
                          gmlp_w_in=gmlp_w_in, gmlp_w_out=gmlp_w_out, diff_w_in=diff_w_in, diff_w_out=diff_w_out,
                          w_sT=w_sT, b_s_row=b_s_row)
            self.kv = dict(k_oth=k_oth, v_oth=v_oth, k_own=k_own, v_own=v_own, q_own=q_own, sg_own=sg_own)

            def load_pass(xin, hm, ec):
                for k4 in range(4):
                    P.dma("sync", "d_x%d" % k4, self.xT[:, k4 * 4:(k4 + 1) * 4, :], xin[:, k4 * 4:(k4 + 1) * 4, :],
                          writes=[("x", k, p) for k in range(k4 * 4, k4 * 4 + 4) for p in range(3)])
                P.dma("sync", "d_hm", self.hmask[:], hm[:, :], writes=["hmask"])
                P.dma("sync", "d_ec", self.ecorr[:], ec[:, :], writes=["ecorr"])

            if mode == "fused":
                load_pass(xa, hmask_a, ecorr_a)
                self.pool_layer(0, 0)
                self.gmlp_layer(1)
                self.dump_pt("dA1")
                self.vcopy(self.xsave[:, :, 0:8], self.xT[:, :, 1016:1024],
                           [("x", k, 1) for k in range(KC)], ["xsave"])
                self.vcopy(self.xsave[:, :, 8:16], self.xT[:, :, 0:8],
                           [("x", k, 0) for k in range(KC)], ["xsave"])
                self.diff_layer(2, rope_a, kv_only=True, kdst=k_oth, vdst=v_oth)
                P.barrier()
                load_pass(xb, hmask_b, ecorr_b)
                self.pool_layer(0, 0)
                self.dump_pt("dB0")
                self.gmlp_layer(1)
                self.dump_pt("dB1")
                self.vcopy(self.xT[:, :, 1168:1184], self.xsave[:, :, :],
                           ["xsave"], [("x", k, 2) for k in range(KC)])
                self.diff_layer(2, rope_b, kv_only=False, kdst=k_own, vdst=v_own)
                self.dump_pt("dB2")
                self.pool_layer(3, 1)
            elif mode == "u1":
                load_pass(xb, hmask_b, ecorr_b)
                self.pool_layer(0, 0)
                self.gmlp_layer(1)
                self.diff_layer(2, rope_b, kv_only=True, kdst=k_oth, vdst=v_oth)
            elif mode == "u2":
                load_pass(xb, hmask_b, ecorr_b)
                self.diff_layer(2, rope_b, kv_only=False, kdst=k_own, vdst=v_own)
                self.pool_layer(3, 1)
            elif mode == "d0":
                load_pass(xb, hmask_b, ecorr_b)
                self.pool_layer(0, 0)
            elif mode == "d1":
                load_pass(xb, hmask_b, ecorr_b)
                self.gmlp_layer(1)
            elif mode == "d2":
                load_pass(xb, hmask_b, ecorr_b)
                self.diff_layer(2, rope_b, kv_only=False, kdst=k_own, vdst=v_own)

            allx = [("x", k, p) for k in range(KC) for p in range(3)]
            for k4 in range(4):
                P.dma("sync", "d_o%d" % k4, out[:, k4 * 4:(k4 + 1) * 4, :], self.xT[:, k4 * 4:(k4 + 1) * 4, 0:1024],
                      reads=allx, writes=[("out", k4)])
            if self.dump:
                P.dma("sync", "d_dump", xdump[:, :, :], self.xT[:], reads=allx, writes=[("out", 9)])
            P.wait_all("sync", [("out", i) for i in range(4)] + [("out", 9)] + self.okeys)
            P.emit_all()
        return nc

    def ada_block_load(self, l, blk):
        wv, wk = self.wload_raw(self.w["ada_w"][l], KC, blk * 256, 256)
        return (l, blk, wv, wk)

    def ada_block_compute(self, item):
        l, blk, wv, wk = item
        g = self.bank(6)
        self.pe([(g[0:2, 0:256], self.scv[:, k, :], wv[:, k, :], k == 0, k == KC - 1) for k in range(KC)],
                [wk, "scv"], [("ps", 6)])
        self.act(self.adarow[0:2, :], g[0:2, 0:256], AF.Identity, [("ps", 6)], ["adarow"])
        t = self.bank(7)
        self.pe([(t[:, c * 2:c * 2 + 2], self.adarow[0:2, c * 128:(c + 1) * 128], self.ident_f[0:2, 0:2], True, True)
                 for c in range(2)], ["adarow", "ident_f"], [("ps", 7)])
        self.tt(self.modall[:, l, blk * 2:blk * 2 + 2, :], t[:, 0:4].rearrange("p (a b) -> p a b", b=2),
                self.abT[:, l, blk * 2:blk * 2 + 2, :], ALU.add, [("ps", 7), "abT"], [("mod", l)])

    def ada_finish(self, l):
        for kind in range(2):
            self.stt(self.modA[:, l, :, kind], self.modall[:, l, 16:32, kind], 1.0, self.ngT[:, l, :],
                     ALU.add, ALU.mult, [("mod", l), "ngT"], [("modA", l)])

    def ada_pump(self):
        if self.ada_inflight is not None:
            self.ada_block_compute(self.ada_inflight)
            self.ada_inflight = None
            if not self.ada_queue:
                self.ada_finish(self.ada_cur)
                self.ada_done.add(self.ada_cur)
                self.ada_cur = None
        if self.ada_queue:
            l, blk = self.ada_queue.pop(0)
            self.ada_inflight = self.ada_block_load(l, blk)

    def ada_schedule(self, l):
        if NO_ADA_BG or self.lite or l in self.ada_done or self.ada_cur == l or l > 3:
            return
        self.ada_cur = l
        self.ada_queue = [(l, blk) for blk in range(24)]

    def ada_layer(self, l):
        if l in self.ada_done:
            return
        P = self.P
        if self.lite:
            self.ada_done.add(l)
            P.op("vector", lambda e: e.memset(self.modall[:, l, :, :], 0.1), [], [("mod", l)])
            P.op("vector", lambda e: e.memset(self.modA[:, l, :, :], 1.0), [], [("modA", l)])
            return
        if self.ada_cur == l:
            while self.ada_cur == l:
                self.ada_pump()
            return
        self.ada_done.add(l)
        for blk in range(24):
            self.ada_block_compute(self.ada_block_load(l, blk))
        self.ada_finish(l)

    def emit_hT(self, l, ranges, hT, sq, rt, rstd, tmp):
        for ri, (c0, c1, kind, d0) in enumerate(ranges):
            n = c1 - c0
            pcs = piece_ids(c0, c1)
            sb = 6 + (ri % 2)
            g = self.bank(sb)
            for kg in range(4):
                s_ = sq[kg % 2]
                self.act(s_[:, :, 0:n], self.xT[:, kg * 4:(kg + 1) * 4, c0:c1], AF.Square,
                         [("x", k, p) for k in range(kg * 4, kg * 4 + 4) for p in pcs], [("sq", kg % 2)])
                self.pe([(g[:, 0:n], self.ones_b[:], s_[:, q, 0:n], kg == 0 and q == 0, kg == 3 and q == 3)
                         for q in range(4)], [("sq", kg % 2), "ones_b"], [("ps", sb)])
            r_ = rt[ri % 2]
            rs_ = rstd[ri % 2]
            self.act(r_[:, 0:n], g[:, 0:n], AF.Sqrt, [("ps", sb)], [("rt", ri % 2)], scale=1.0 / D, bias=EPS)
            self.recip(rs_[:, 0:n], r_[:, 0:n], [("rt", ri % 2)], [("rstd", ri % 2)])
            for k in range(KC):
                t_ = tmp[k % 2]
                self.stt(t_[:, 0:n], self.xT[:, k, c0:c1], self.modA[:, l, k, kind:kind + 1], rs_[:, 0:n],
                         ALU.mult, ALU.mult, [("x", k, p) for p in pcs] + [("rstd", ri % 2), ("modA", l)],
                         [("tmp", k % 2)])
                self.act(hT[:, k, d0:d0 + n], t_[:, 0:n], AF.Identity, [("tmp", k % 2), ("mod", l)], ["hT"],
                         bias=self.modall[:, l, k, kind:kind + 1], scale=1.0)

    def hT_scratch(self):
        sq = [self.carve([128, 4, 512], BF16) for _ in range(2)]
        rt = [self.carve([128, 512], F32) for _ in range(2)]
        rstd = [self.carve([128, 512], F32) for _ in range(2)]
        tmp = [self.carve([128, 512], F32) for _ in range(2)]
        return sq, rt, rstd, tmp

    def next_set(self):
        b0 = 3 * (self.ps_rr % 2)
        self.ps_rr += 1
        return b0

    def mm_cols(self, b0, pieces, wv, cc, rhs_of, nk):
        mms = []
        for k in range(nk):
            for pi, (a, b) in enumerate(pieces):
                mms.append((self.bank(b0 + pi)[:, 0:b - a], wv[:, k, cc * 128:(cc + 1) * 128], rhs_of(k, a, b),
                            k == 0, k == nk - 1))
        return mms

    def residual(self, l, mc, b0, pieces_kind):
        for pi, (a, b, kind) in enumerate(pieces_kind):
            pcs = piece_ids(a, b)
            self.stt(self.xT[:, mc, a:b], self.bank(b0 + pi)[:, 0:b - a], self.modall[:, l, 32 + mc, kind:kind + 1],
                     self.xT[:, mc, a:b], ALU.mult, ALU.add,
                     [("ps", b0 + pi), ("mod", l)] + [("x", mc, p) for p in pcs], [("x", mc, p) for p in pcs])

    def pool_layer(self, l, j):
        P = self.P
        self.ada_layer(l)
        self.ada_schedule(l + 1)
        P.barrier()
        self.arena_reset()
        hT = self.carve([128, KC, NT], BF16)
        pbuf = self.carve([128, 8, NIN], BF16)
        sgb = self.carve([128, 8, NIN], BF16)
        mark = self.a_off
        sq, rt, rstd, tmp = self.hT_scratch()
        self.emit_hT(l, [(0, 512, 0, 0), (512, 1024, 0, 512), (1024, 1168, 1, 1024), (1168, 1184, 0, 1168)],
                     hT, sq, rt, rstd, tmp)
        P.barrier()
        self.a_off = mark
        zext = [self.carve([128, NT], F32) for _ in range(2)]
        abuf = [self.carve([128, NT], F32) for _ in range(2)]
        w_in = self.w["pool_w_in"][j]
        w_out = self.w["pool_w_out"][j]
        hm = self.hmask
        ec = self.ecorr[:].rearrange("p (w e t) -> p w e t", e=4, t=8)
        for grp in range(4):
            win = 2 ** (grp + 1)
            hw = win // 2
            for zb in range(4):
                wv, wk = self.wload(w_in, KC, (grp * 8 + zb * 2) * 128, 256)
                for cc in range(2):
                    zc = zb * 2 + cc
                    b0 = self.next_set()
                    self.pe(self.mm_cols(b0, PIECES, wv, cc, lambda k, a, b: hT[:, k, a:b], KC),
                            [wk, "hT"], [("ps", b0), ("ps", b0 + 1), ("ps", b0 + 2)])
                    zi = zc % 2
                    zx = zext[zi]
                    zk = ("zx", zi)
                    b2 = self.bank(b0 + 2)
                    self.act(zx[:, 8:520], self.bank(b0)[:, 0:512], AF.Identity, [("ps", b0)], [zk])
                    self.act(zx[:, 520:1032], self.bank(b0 + 1)[:, 0:512], AF.Identity, [("ps", b0 + 1)], [zk])
                    self.act(zx[:, 1048:1176], b2[:, 0:128], AF.Identity, [("ps", b0 + 2)], [zk])
                    for (dst, src, mo) in ((1040, 128, 0), (1176, 136, 8), (0, 144, 16), (1032, 152, 24)):
                        self.act(zx[:, dst:dst + 8], b2[:, src:src + 8], AF.Identity, [("ps", b0 + 2), "hmask"], [zk],
                                 scale=hm[:, mo:mo + 1])
                    a_, b_ = abuf
                    self.tt(a_[:, 0:1183], zx[:, 0:1183], zx[:, 1:1184], ALU.add, [zk], ["ab0"])
                    fb, fk = a_, "ab0"
                    if win >= 4:
                        self.tt(b_[:, 0:1181], a_[:, 0:1181], a_[:, 2:1183], ALU.add, ["ab0"], ["ab1"])
                        fb, fk = b_, "ab1"
                    if win >= 8:
                        self.tt(a_[:, 0:1177], b_[:, 0:1177], b_[:, 4:1181], ALU.add, ["ab1"], ["ab0"])
                        fb, fk = a_, "ab0"
                    if win >= 16:
                        self.tt(b_[:, 0:1169], a_[:, 0:1169], a_[:, 8:1177], ALU.add, ["ab0"], ["ab1"])
                        fb, fk = b_, "ab1"
                    for ei, base in enumerate((8 - hw, 8 - hw + 1016, 1048 - hw, 1048 - hw + 120)):
                        self.tt(fb[:, base:base + 8], fb[:, base:base + 8], ec[:, grp, ei, :], ALU.mult,
                                [fk, "ecorr"], [fk])
                    self.stt(pbuf[:, zc, 0:1024], fb[:, 8 - hw:8 - hw + 1024], 1.0 / win, zx[:, 8:1032],
                             ALU.mult, ALU.subtract, [fk, zk], [("p", zc)])
                    self.stt(pbuf[:, zc, 1024:1152], fb[:, 1048 - hw:1048 - hw + 128], 1.0 / win, zx[:, 1048:1176],
                             ALU.mult, ALU.subtract, [fk, zk], [("p", zc)])
            for gb in range(4):
                wv, wk = self.wload(w_in, KC, 4096 + (grp * 8 + gb * 2) * 128, 256)
                for cc in range(2):
                    gc = gb * 2 + cc
                    b0 = self.next_set()
                    self.pe(self.mm_cols(b0, INNER, wv, cc, lambda k, a, b: hT[:, k, a:b], KC),
                            [wk, "hT"], [("ps", b0), ("ps", b0 + 1), ("ps", b0 + 2)])
                    for pi, (a, b) in enumerate(INNER):
                        self.act(sgb[:, gc, a:b], self.bank(b0 + pi)[:, 0:b - a], AF.Silu, [("ps", b0 + pi)], [("sg", gc)])
            for nb in range(2):
                wv, wk = self.wload(self.w["pool_w_grp"][j][grp], 8, nb * 512, 512)
                for cc in range(4):
                    mc = nb * 4 + cc
                    b0 = self.next_set()
                    self.pe(self.mm_cols(b0, INNER, wv, cc, lambda k, a, b: pbuf[:, k, a:b], 8),
                            [wk] + [("p", k) for k in range(8)], [("ps", b0), ("ps", b0 + 1), ("ps", b0 + 2)])
                    for pi, (a, b) in enumerate(INNER):
                        self.stt(sgb[:, mc, a:b], self.bank(b0 + pi)[:, 0:b - a], self.pscT[:, j, grp * 8 + mc:grp * 8 + mc + 1],
                                 sgb[:, mc, a:b], ALU.mult, ALU.mult, [("ps", b0 + pi), ("sg", mc), "pscT"], [("sg", mc)])
            wo = w_out[grp * 1024:(grp + 1) * 1024, :]
            for nb in range(4):
                wv, wk = self.wload(wo, 8, nb * 512, 512)
                for cc in range(4):
                    mc = nb * 4 + cc
                    b0 = self.next_set()
                    self.pe(self.mm_cols(b0, INNER, wv, cc, lambda k, a, b: sgb[:, k, a:b], 8),
                            [wk] + [("sg", k) for k in range(8)], [("ps", b0), ("ps", b0 + 1), ("ps", b0 + 2)])
                    self.residual(l, mc, b0, [(0, 512, 0), (512, 1024, 0), (1024, 1152, 1)])

    def gmlp_layer(self, l):
        P = self.P
        self.ada_layer(l)
        self.ada_schedule(l + 1)
        P.barrier()
        self.arena_reset()
        wsf2 = self.carve([128, 1024], F32)
        wsb2 = self.carve([128, 1024], BF16)
        Rb2 = self.carve([128, 1024], F32)
        bsb2 = self.carve([128, 1024], F32)
        bsrow = self.carve([128, 1024], F32)
        wsb = wsb2.rearrange("p (a b) -> p a b", b=128)
        Rb = Rb2.rearrange("p (a b) -> p a b", b=128)
        bsb = bsb2.rearrange("p (a b) -> p a b", b=128)
        P.dma("sync", "d_g0", wsf2, self.w["w_sT"].rearrange("p a b -> p (a b)"), writes=["wsf"])
        P.dma("sync", "d_g1", bsrow[0:1, :], self.w["b_s_row"][:, :], writes=["bsrow"])
        self.vcopy(wsb2, wsf2, ["wsf"], ["wsb"])
        for half in range(2):
            g = self.bank(6 + half)
            self.pe([(g[:, 0:512], self.ones_b[:], wsb2[:, half * 512:(half + 1) * 512], True, True)],
                    ["wsb", "ones_b"], [("ps", 6 + half)])
            self.vcopy(Rb2[:, half * 512:(half + 1) * 512], g[:, 0:512], [("ps", 6 + half)], ["Rb"])
        for half in range(2):
            g = self.bank(6 + half)
            self.pe([(g[:, 0:512], self.ones_f[0:1, :], bsrow[0:1, half * 512:(half + 1) * 512], True, True)],
                    ["bsrow", "ones_f"], [("ps", 6 + half)])
            self.vcopy(bsb2[:, half * 512:(half + 1) * 512], g[:, 0:512], [("ps", 6 + half)], ["bsb"])
        base = self.a_off
        w_in = self.w["gmlp_w_in"]
        w_out = self.w["gmlp_w_out"]
        lng = self.lngb[:, 0, :]
        lnb = self.lngb[:, 1, :]
        subpasses = [
            dict(ranges=[(0, 512, 0, 0), (1024, 1152, 1, 512)], nt=5),
            dict(ranges=[(512, 1024, 0, 0)], nt=4),
        ]
        for sp in subpasses:
            P.barrier()
            self.a_off = base
            nt = sp["nt"]
            ncol = nt * 128
            hs = self.carve([128, KC, ncol], BF16)
            vsm = self.carve([128, nt, 4096], BF16)
            junk = self.carve([128, 256], BF16)
            s1p = self.carve([128, nt, 16], F32)
            s2p = self.carve([128, nt, 16], F32)
            st = self.carve([128, nt, 8], F32)
            mark = self.a_off
            sq, rt, rstd, tmp = self.hT_scratch()
            self.emit_hT(l, sp["ranges"], hs, sq, rt, rstd, tmp)
            P.barrier()
            self.a_off = mark
            Bj = [self.carve([128, 128], F32) for _ in range(2)]
            ub = [self.carve([128, ncol], F32) for _ in range(2)]
            gbuf = [self.carve([128, ncol], F32) for _ in range(2)]
            cpieces = [(0, 512), (512, ncol)] if ncol > 512 else [(0, 512)]
            for vb in range(16):
                wv, wk = self.wload(w_in, KC, 4096 + vb * 256, 256)
                for ti in range(nt):
                    bk = self.st_rr % 4
                    self.st_rr += 1
                    g = self.bank(bk)
                    self.pe([(g[:, 0:256], hs[:, k, ti * 128:(ti + 1) * 128], wv[:, k, :], k == 0, k == KC - 1)
                             for k in range(KC)], [wk, "hT"], [("ps", bk)])
                    vk2 = [("v", ti, 2 * vb), ("v", ti, 2 * vb + 1)]
                    self.act(vsm[:, ti, vb * 256:(vb + 1) * 256], g[:, 0:256], AF.Gelu_apprx_tanh,
                             [("ps", bk)], vk2 + [("s1p", ti)], accum_out=s1p[:, ti, vb:vb + 1])
                    self.act(junk, vsm[:, ti, vb * 256:(vb + 1) * 256], AF.Square,
                             vk2, ["junk", ("s2p", ti)], accum_out=s2p[:, ti, vb:vb + 1])
            for ti in range(nt):
                sk = ("st", ti)
                self.P.op("vector", lambda e, o_=st[:, ti, 0:1], i_=s1p[:, ti, :]: e.reduce_sum(out=o_, in_=i_, axis=mybir.AxisListType.X),
                          [("s1p", ti)], [sk])
                self.P.op("vector", lambda e, o_=st[:, ti, 1:2], i_=s2p[:, ti, :]: e.reduce_sum(out=o_, in_=i_, axis=mybir.AxisListType.X),
                          [("s2p", ti)], [sk])
                self.ts(st[:, ti, 2:3], st[:, ti, 0:1], 1.0 / 4096, None, ALU.mult, None, [sk], [sk])
                self.tt(st[:, ti, 3:4], st[:, ti, 2:3], st[:, ti, 2:3], ALU.mult, [sk], [sk])
                self.stt(st[:, ti, 4:5], st[:, ti, 1:2], 1.0 / 4096, st[:, ti, 3:4], ALU.mult, ALU.subtract, [sk], [sk])
                self.act(st[:, ti, 5:6], st[:, ti, 4:5], AF.Sqrt, [sk], [sk], bias=EPS, scale=1.0)
                self.recip(st[:, ti, 6:7], st[:, ti, 5:6], [sk], [sk])
                vall = [("v", ti, jj) for jj in range(32)]
                self.ts(vsm[:, ti, :], vsm[:, ti, :], st[:, ti, 2:3], st[:, ti, 6:7], ALU.subtract, ALU.mult,
                        [sk] + vall, vall)
            def spatial(jc, Bj=None, vsm=None, nt=None):
                h = jc // 4
                bj = Bj[jc % 2]
                self.stt(bj, Rb[:, h, :], lnb[:, jc:jc + 1], bsb[:, h, :], ALU.mult, ALU.add,
                         ["Rb", "bsb", "lngb"], [("Bj", jc % 2)])
                bk = 6 + (jc % 2)
                bk2 = 2 + 3 * (jc % 2)
                mms = []
                for ti in range(nt):
                    o = self.bank(bk)[:, ti * 128:(ti + 1) * 128] if ti < 4 else self.bank(bk2)[:, 0:128]
                    mms.append((o, vsm[:, ti, jc * 128:(jc + 1) * 128], wsb[:, h, :], True, True))
                self.pe(mms, [("v", ti, jc) for ti in range(nt)] + ["wsb"], [("ps", bk), ("ps", bk2)])
                for ti in range(nt):
                    o = self.bank(bk)[:, ti * 128:(ti + 1) * 128] if ti < 4 else self.bank(bk2)[:, 0:128]
                    self.stt(vsm[:, ti, jc * 128:(jc + 1) * 128], o, lng[:, jc:jc + 1], bj, ALU.mult, ALU.add,
                             [("ps", bk), ("ps", bk2), ("Bj", jc % 2), "lngb"], [("v", ti, jc)])
            allv = [("v", ti, jj) for ti in range(nt) for jj in range(32)]
            for ub_i in range(16):
                wu, wuk = self.wload(w_in, KC, ub_i * 256, 256)
                for cc in range(2):
                    jc = ub_i * 2 + cc
                    b0 = self.next_set()
                    self.pe(self.mm_cols(b0, cpieces, wu, cc, lambda k, a, b: hs[:, k, a:b], KC),
                            [wuk, "hT"], [("ps", b0), ("ps", b0 + 1)])
                    for pi, (a, b) in enumerate(cpieces):
                        self.act(ub[cc][:, a:b], self.bank(b0 + pi)[:, 0:b - a], AF.Gelu_apprx_tanh,
                                 [("ps", b0 + pi)], [("ub", cc)])
                wg, wgk = self.wload(w_in, KC, 8192 + ub_i * 256, 256)
                for cc in range(2):
                    jc = ub_i * 2 + cc
                    b0 = self.next_set()
                    self.pe(self.mm_cols(b0, cpieces, wg, cc, lambda k, a, b: hs[:, k, a:b], KC),
                            [wgk, "hT"], [("ps", b0), ("ps", b0 + 1)])
                    for pi, (a, b) in enumerate(cpieces):
                        self.act(gbuf[cc][:, a:b], self.bank(b0 + pi)[:, 0:b - a], AF.Silu,
                                 [("ps", b0 + pi)], [("gb", cc)])
                    spatial(jc, Bj=Bj, vsm=vsm, nt=nt)
                    self.tt(gbuf[cc][:, 0:ncol], gbuf[cc][:, 0:ncol], ub[cc][:, 0:ncol], ALU.mult,
                            [("gb", cc), ("ub", cc)], [("gb", cc)])
                    self.tt(vsm[:, :, jc * 128:(jc + 1) * 128], vsm[:, :, jc * 128:(jc + 1) * 128],
                            gbuf[cc][:, 0:ncol].rearrange("p (t q) -> p t q", q=128), ALU.mult,
                            [("gb", cc)] + [("v", ti, jc) for ti in range(nt)], [("v", ti, jc) for ti in range(nt)])
            pk = []
            for (c0, c1, kind, d0) in sp["ranges"]:
                pk.append((c0, c1, kind, d0))
            for mc in range(KC):
                wv, wk = self.wload(w_out, 32, mc * 128, 128)
                b0 = self.next_set()
                mms = []
                for k in range(32):
                    for pi, (c0, c1, kind, d0) in enumerate(pk):
                        t0 = d0 // 128
                        ntile = (c1 - c0) // 128
                        mms.append((self.bank(b0 + pi)[:, 0:c1 - c0].rearrange("p (t q) -> p t q", q=128), wv[:, k, :],
                                    vsm[:, t0:t0 + ntile, k * 128:(k + 1) * 128], k == 0, k == 31))
                self.pe(mms, [wk] + allv, [("ps", b0 + pi) for pi in range(len(pk))])
                self.residual(l, mc, b0, [(c0, c1, kind) for (c0, c1, kind, d0) in pk])

    def diff_layer(self, l, rope_d, kv_only, kdst, vdst):
        P = self.P
        self.ada_layer(l)
        if not kv_only:
            self.ada_schedule(l + 1)
        P.barrier()
        self.arena_reset()
        lam_init = 0.8 - 0.6 * math.exp(-0.3 * l)
        dsm = self.dsm
        dv = self.dv
        self.tt(dsm[:, 0:1], dv[:, 2:3], dv[:, 3:4], ALU.mult, ["dv"], ["dsm"])
        self.tt(dsm[:, 1:2], dv[:, 4:5], dv[:, 5:6], ALU.mult, ["dv", "dsm"], ["dsm"])
        g = self.bank(6)
        self.pe([(g[:, 0:2], self.ones_f[:], dsm[:, 0:2], True, True)], ["dsm", "ones_f"], [("ps", 6)])
        self.act(dsm[:, 2:4], g[:, 0:2], AF.Exp, [("ps", 6), "dsm"], ["dsm"])
        self.tt(dsm[:, 4:5], dsm[:, 2:3], dsm[:, 3:4], ALU.subtract, ["dsm"], ["dsm"])
        self.ts(dsm[:, 5:6], dsm[:, 4:5], lam_init, -1.0, ALU.add, ALU.mult, ["dsm"], ["dsm"])
        self.ts(dsm[:, 6:8], dv[:, 6:8], 1.0 - lam_init, None, ALU.mult, None, ["dv", "dsm"], ["dsm"])
        neglam = dsm[:, 5:6]
        if self.stop == "LAM":
            return

        hT = self.carve([128, KC, NT], BF16)
        rope = self.carve([128, 2, NT], F32)
        P.dma("sync", "d_rope", rope, rope_d[:, :, :], writes=["rope"])
        sqb = [self.carve([128, 512], BF16) for _ in range(6)]
        qgb = [self.carve([128, 512], BF16) for _ in range(6)]
        rsb = [self.carve([128, 512], F32) for _ in range(3)]
        kst = [self.carve([128, NT], BF16) for _ in range(2)]
        vst = self.carve([128, 9, 256], BF16)
        mark = self.a_off
        sq, rt, rstd, tmp = self.hT_scratch()
        if kv_only:
            ranges = [(0, 512, 0, 0), (512, 1024, 0, 512), (1024, 1152, 1, 1024)]
        else:
            ranges = [(0, 512, 0, 0), (512, 1024, 0, 512), (1024, 1152, 1, 1024), (1168, 1184, 0, 1168)]
            P.op("vector", lambda e: e.memset(hT[:, :, 1152:1168], 0.0), [], ["hT"])
        self.emit_hT(l, ranges, hT, sq, rt, rstd, tmp)
        P.barrier()
        self.a_off = mark
        t1b = [self.carve([128, 512], F32) for _ in range(3)]
        t2b = [self.carve([128, 512], F32) for _ in range(3)]
        if self.stop == "HT":
            return
        w_in = self.w["diff_w_in"]
        proj_pieces = INNER if kv_only else PIECES
        self.rr2 = 0
        self.rr3 = 0

        def qk_proj(cc, wv, wk, gcol):
            b0 = self.next_set()
            self.pe(self.mm_cols(b0, proj_pieces, wv, cc, lambda k, a, b: hT[:, k, a:b], KC),
                    [wk, "hT"], [("ps", b0), ("ps", b0 + 1), ("ps", b0 + 2)])
            slots = []
            for pi, (a, b) in enumerate(proj_pieces):
                n = b - a
                i6 = self.rr2 % 6
                self.rr2 += 1
                slots.append(i6)
                src = self.bank(b0 + pi)[:, 0:n]
                self.act(sqb[i6][:, 0:n], src, AF.Square, [("ps", b0 + pi)], [("sqb", i6)])
                self.act(qgb[i6][:, 0:n], src, AF.Identity, [("ps", b0 + pi), "dv"], [("qgb", i6)], scale=gcol)
            return b0, slots

        def qk_chain(b0, slots, dst_fn):
            pcs_ = list(enumerate(proj_pieces))
            for pi, (a, b) in pcs_:
                n = b - a
                self.pe([(self.bank(b0 + pi)[:, 0:n], self.ones_b[:], sqb[slots[pi]][:, 0:n], True, True)],
                        [("sqb", slots[pi]), "ones_b"], [("ps", b0 + pi)])
            for pi, (a, b) in pcs_:
                n = b - a
                self.act(rsb[pi][:, 0:n], self.bank(b0 + pi)[:, 0:n], AF.Sqrt, [("ps", b0 + pi)], [("rsb", pi)],
                         scale=1.0 / 128, bias=EPS)
            for pi, (a, b) in pcs_:
                n = b - a
                self.pe([(self.bank(b0 + pi)[:, 0:n], self.perm_b[:], qgb[slots[pi]][:, 0:n], True, True)],
                        [("qgb", slots[pi]), "perm_b"], [("ps", b0 + pi)])
            for pi, (a, b) in pcs_:
                n = b - a
                self.recip(rsb[pi][:, 0:n], rsb[pi][:, 0:n], [("rsb", pi)], [("rsb", pi)])
            for pi, (a, b) in pcs_:
                n = b - a
                self.tt(t2b[pi][:, 0:n], self.bank(b0 + pi)[:, 0:n], rope[:, 1, a:b], ALU.mult,
                        [("ps", b0 + pi), "rope"], [("t2b", pi)])
            for pi, (a, b) in pcs_:
                n = b - a
                self.tt(t1b[pi][:, 0:n], qgb[slots[pi]][:, 0:n], rope[:, 0, a:b], ALU.mult,
                        [("qgb", slots[pi]), "rope"], [("t1b", pi)])
            for pi, (a, b) in pcs_:
                n = b - a
                self.tt(t1b[pi][:, 0:n], t1b[pi][:, 0:n], t2b[pi][:, 0:n], ALU.add, [("t1b", pi), ("t2b", pi)], [("t1b", pi)])
            for pi, (a, b) in pcs_:
                n = b - a
                dst, dk = dst_fn(a, b)
                self.tt(dst, t1b[pi][:, 0:n], rsb[pi][:, 0:n], ALU.mult, [("t1b", pi), ("rsb", pi)], [dk])

        def run_qk(col_base, gcol, store):
            st_ = {}

            def proj(ch):
                if ch % 2 == 0:
                    st_["w"] = self.wload(w_in, KC, col_base + (ch // 2) * 256, 256)
                wv, wk = st_["w"]
                return qk_proj(ch % 2, wv, wk, gcol)

            pend = proj(0)
            for ch in range(16):
                nxt = proj(ch + 1) if ch + 1 < 16 else None
                ks = kst[ch % 2]
                kk = ("kst", ch % 2)
                qk_chain(pend[0], pend[1], lambda a, b, ks=ks, kk=kk: (ks[:, a:b], kk))
                store(ch, ks, kk)
                pend = nxt

        run_qk(2048, dv[:, 1:2], lambda ch, ks, kk: P.dma(
            "sync", "d_k%d" % (ch % 2), kdst[ch * 128:(ch + 1) * 128, :], ks[:, 0:NIN], reads=[kk], writes=[self.okey()]))
        if self.stop == "K":
            return
        for vb in range(8):
            wv, wk = self.wload(w_in, KC, 4096 + vb * 256, 256)
            for ti in range(9):
                bk = self.st_rr % 4
                self.st_rr += 1
                g = self.bank(bk)
                self.pe([(g[:, 0:256], hT[:, k, ti * 128:(ti + 1) * 128], wv[:, k, :], k == 0, k == KC - 1)
                         for k in range(KC)], [wk, "hT"], [("ps", bk)])
                self.act(vst[:, ti, :], g[:, 0:256], AF.Identity, [("ps", bk)], ["vst"])
            P.dma("sync", "d_v", vdst.rearrange("(t p) e -> p t e", p=128)[:, :, vb * 256:(vb + 1) * 256], vst,
                  reads=["vst"], writes=[self.okey()])
        if kv_only or self.stop == "V":
            return
        q_own = self.kv["q_own"]
        sg_own = self.kv["sg_own"]

        run_qk(0, dv[:, 0:1], lambda ch, ks, kk: P.dma(
            "sync", "d_k%d" % (ch % 2), q_own[ch * 128:(ch + 1) * 128, :], ks[:, 0:NT], reads=[kk], writes=[self.okey()]))
        for gb_ in range(8):
            wv, wk = self.wload(w_in, KC, 6144 + gb_ * 256, 256)
            for cc in range(2):
                ch = gb_ * 2 + cc
                ks = kst[ch % 2]
                kk = ("kst", ch % 2)
                b0 = self.next_set()
                self.pe(self.mm_cols(b0, PIECES, wv, cc, lambda k, a, b: hT[:, k, a:b], KC),
                        [wk, "hT"], [("ps", b0), ("ps", b0 + 1), ("ps", b0 + 2)])
                for pi, (a, b) in enumerate(PIECES):
                    self.act(ks[:, a:b], self.bank(b0 + pi)[:, 0:b - a], AF.Silu, [("ps", b0 + pi)], [kk])
                P.dma("sync", "d_k%d" % (ch % 2), sg_own[ch * 128:(ch + 1) * 128, :], ks[:, 0:NT],
                      reads=[kk], writes=[self.okey()])

        if self.stop == "QS":
            return
        P.barrier()
        self.arena_reset()
        NQ = 1040
        ybuf = self.carve([128, KC, NQ], BF16)
        qh = [self.carve([128, 2, NT], BF16) for _ in range(2)]
        sgh = [self.carve([128, 2, NT], BF16) for _ in range(2)]
        kh = [self.carve([128, 2, 2 * NIN], BF16) for _ in range(2)]
        vh = [self.carve([128, 18, 272], BF16) for _ in range(2)]
        pT = [self.carve([128, 2, 256], BF16) for _ in range(4)]
        osb = [self.carve([128, 256], F32) for _ in range(2)]
        o2b = [self.carve([128, 256], F32) for _ in range(2)]
        obf = [self.carve([128, 256], BF16) for _ in range(2)]
        ojk = self.carve([128, 256], BF16)
        sm = self.carve([128, 2, 16], F32)
        zt = self.carve([128, 256], F32)
        P.op("vector", lambda e: e.memset(zt, 0.0), [], ["zt"])
        for i in range(2):
            P.op("vector", lambda e, i=i: e.memset(vh[i][:, :, 256:258], 1.0), [], [("vh1", i)])
        k_oth, v_oth = self.kv["k_oth"], self.kv["v_oth"]
        kd = [kdst, k_oth]
        vd = [vdst, v_oth]
        qblocks = [(0, 256, 0), (256, 512, 256), (512, 768, 512), (768, 1024, 768), (1168, 1184, 1024)]
        scale = 128 ** -0.5
        pti = 0
        epi = 0
        deferred = []
        for h in range(8):
            hb = h % 2
            rows = slice(2 * h * 128, (2 * h + 2) * 128)
            gq = [("qh", hb), ("sgh", hb)]
            gk = [("kh", hb), ("vh", hb)]
            P.dma("sync", "d_qh%d" % hb, qh[hb], q_own[rows, :].rearrange("(c d) t -> d c t", d=128),
                  reads=[], writes=gq)
            P.dma("sync", "d_qh%d" % hb, sgh[hb], sg_own[rows, :].rearrange("(c d) t -> d c t", d=128),
                  reads=[], writes=gq)
            for r in range(2):
                P.dma("sync", "d_kh%d" % hb, kh[hb][:, :, r * NIN:(r + 1) * NIN],
                      kd[r][rows, :].rearrange("(c d) t -> d c t", d=128), reads=[], writes=gk)
                P.dma("sync", "d_kh%d" % hb, vh[hb][:, r * 9:(r + 1) * 9, 0:256],
                      vd[r].rearrange("(t p) e -> p t e", p=128)[:, :, h * 256:(h + 1) * 256],
                      reads=[("vh1", hb)], writes=gk)
            for (q0, q1, y0) in qblocks:
                nq = q1 - q0
                nsub = (nq + 127) // 128
                accb = [[0 + 2 * s_ + i for i in range(2)] for s_ in range(nsub)]
                SB = (4, 5, 7)

                def s_issue(kt, q0=q0, q1=q1, nq=nq, hb=hb):
                    sbk = SB[kt % 3]
                    S = self.bank(sbk)
                    self.pe([(S[:, i * 256:i * 256 + nq], kh[hb][:, i, kt * 128:(kt + 1) * 128], qh[hb][:, i, q0:q1], True, True)
                             for i in range(2)], [("kh", hb), ("qh", hb)], [("ps", sbk)])

                s_issue(0)
                s_issue(1)
                for kt in range(18):
                    if kt == 3:
                        for f_ in deferred:
                            f_()
                        deferred.clear()
                    if kt + 2 < 18:
                        s_issue(kt + 2)
                    sbk = SB[kt % 3]
                    S = self.bank(sbk)
                    pt = pT[pti % 4]
                    pk_ = ("pT", pti % 4)
                    pti += 1
                    self.act(pt[:, :, 0:nq], S[:, 0:512].rearrange("p (i q) -> p i q", q=256)[:, :, 0:nq], AF.Exp,
                             [("ps", sbk)], [pk_], scale=scale)
                    mms = []
                    for s_ in range(nsub):
                        m = min(128, nq - s_ * 128)
                        for i in range(2):
                            mms.append((self.bank(accb[s_][i])[0:m, 0:258], pt[:, i, s_ * 128:s_ * 128 + m],
                                        vh[hb][:, kt, 0:258], kt == 0, kt == 17))
                    self.pe(mms, [pk_, ("vh", hb)], [("ps", accb[s_][i]) for s_ in range(nsub) for i in range(2)])
                eps_ = []
                for s_ in range(nsub):
                    m = min(128, nq - s_ * 128)
                    e2 = epi % 2
                    epi += 1
                    eps_.append((s_, m, e2))
                    O0 = self.bank(accb[s_][0])
                    O1 = self.bank(accb[s_][1])
                    sk = ("sm", e2)
                    smv = sm[:, e2, :]
                    self.recip(smv[0:m, 0:1], O0[0:m, 256:257], [("ps", accb[s_][0])], [sk])
                    self.recip(smv[0:m, 1:2], O1[0:m, 256:257], [("ps", accb[s_][1])], [sk])
                    self.tt(smv[0:m, 2:3], smv[0:m, 1:2], neglam[0:m, :], ALU.mult, [sk, "dsm"], [sk])
                    self.stt(o2b[e2][0:m, :], O1[0:m, 0:256], smv[0:m, 2:3], zt[0:m, :], ALU.mult, ALU.add,
                             [("ps", accb[s_][1]), sk, "zt"], [("o2b", e2)])
                    self.stt(osb[e2][0:m, :], O0[0:m, 0:256], smv[0:m, 0:1], o2b[e2][0:m, :], ALU.mult, ALU.add,
                             [("ps", accb[s_][0]), sk, ("o2b", e2)], [("osb", e2)])
                def ep2(eps_=eps_, h=h, hb=hb, q0=q0, y0=y0):
                  for (s_, m, e2) in eps_:
                      sk = ("sm", e2)
                      smv = sm[:, e2, :]
                      self.act(ojk[0:m, :], osb[e2][0:m, :], AF.Square, [("osb", e2)], ["ojk", sk], accum_out=smv[0:m, 3:4])
                      self.act(smv[0:m, 4:5], smv[0:m, 3:4], AF.Sqrt, [sk], [sk], scale=1.0 / 256, bias=EPS)
                      self.recip(smv[0:m, 6:7], smv[0:m, 4:5], [sk], [sk])
                      if m < 128:
                          P.op("vector", lambda e, e2=e2: e.memset(obf[e2][:, :], 0.0), [], [("obf", e2)])
                      self.ts(obf[e2][0:m, :], osb[e2][0:m, :], smv[0:m, 6:7], None, ALU.mult, None,
                              [("osb", e2), sk], [("obf", e2)])
                      tb = 6
                      T = self.bank(tb, BF16)[:, e2 * 256:(e2 + 1) * 256]
                      self.pet([(T[:, ec * 128:(ec + 1) * 128], obf[e2][:, ec * 128:(ec + 1) * 128], self.ident_b[:])
                                for ec in range(2)], [("obf", e2), "ident_b"], [("ps", tb)])
                      c0 = q0 + s_ * 128
                      yc = y0 + s_ * 128
                      for ec in range(2):
                          self.stt(ybuf[:, 2 * h + ec, yc:yc + m], T[:, ec * 128:ec * 128 + m], dsm[:, 6 + ec:7 + ec],
                                   sgh[hb][:, ec, c0:c0 + m], ALU.mult, ALU.mult,
                                   [("ps", tb), "dsm", ("sgh", hb)], [("y", 2 * h + ec)])

                deferred.append(ep2)
        for f_ in deferred:
            f_()
        deferred.clear()
        if self.stop == "ATT":
            return
        w_out = self.w["diff_w_out"]
        ypieces = [(0, 512), (512, 1024), (1024, 1040)]
        xcols = [(0, 512, 0), (512, 1024, 0), (1168, 1184, 0)]
        for nb in range(8):
            wv, wk = self.wload(w_out, KC, nb * 256, 256)
            for cc in range(2):
                mc = nb * 2 + cc
                b0 = self.next_set()
                self.pe(self.mm_cols(b0, ypieces, wv, cc, lambda k, a, b: ybuf[:, k, a:b], KC),
                        [wk] + [("y", k) for k in range(KC)], [("ps", b0), ("ps", b0 + 1), ("ps", b0 + 2)])
                self.residual(l, mc, b0, xcols)


def _cols(x_b, ctx_b, s):
    L, C = x_b.shape[0], ctx_b.shape[0]
    hx, hc = L // 2, C // 2
    rows = np.zeros((NT, D), np.float32)
    rows[0:1024] = x_b[s * hx:(s + 1) * hx]
    rows[1024:1152] = ctx_b[s * hc:(s + 1) * hc]

    def halo(src, lo, n, dst):
        a, b = max(lo, 0), min(lo + n, src.shape[0])
        if b > a:
            rows[dst + (a - lo):dst + (b - lo)] = src[a:b]

    halo(ctx_b, s * hc - 8, 8, 1152)
    halo(ctx_b, (s + 1) * hc, 8, 1160)
    halo(x_b, s * hx - 8, 8, 1168)
    halo(x_b, (s + 1) * hx, 8, 1176)
    return np.ascontiguousarray(rows.reshape(NT, KC, 128).transpose(2, 1, 0))


def _masks(s):
    hm = np.zeros((128, 32), np.float32)
    hm[:, 0:8] = 1.0 if s == 1 else 0.0
    hm[:, 8:16] = 1.0 if s == 0 else 0.0
    hm[:, 16:24] = 1.0 if s == 1 else 0.0
    hm[:, 24:32] = 1.0 if s == 0 else 0.0
    ec = np.ones((4, 4, 8), np.float32)
    for gi, w in enumerate((2, 4, 8, 16)):
        for seg, (L, half) in enumerate(((2048, 1024), (256, 128))):
            for side in range(2):
                for t8 in range(8):
                    t = s * half + (t8 if side == 0 else half - 8 + t8)
                    lo = max(t - w // 2, 0)
                    hi = min(t + w - w // 2, L)
                    ec[gi, seg * 2 + side, t8] = w / float(hi - lo)
    ecb = np.broadcast_to(ec.reshape(1, 128), (128, 128)).astype(np.float32).copy()
    return hm, ecb


def _rope(s):
    pos = np.zeros(NT, np.int64)
    valid = np.zeros(NT, bool)
    pos[0:1024] = s * 1024 + np.arange(1024)
    valid[0:1024] = True
    pos[1168:1176] = s * 1024 - 8 + np.arange(8)
    pos[1176:1184] = (s + 1) * 1024 + np.arange(8)
    valid[1168:1184] = True
    pos = np.clip(pos, 0, 2047)
    row = (pos // 64).astype(np.float32)
    col = (pos % 64).astype(np.float32)
    inv = (np.float32(10000.0) ** (-np.arange(0, 64, 2, dtype=np.float32) / np.float32(64))).astype(np.float32)
    d = np.arange(128)
    ang = np.where((d < 64)[:, None], row[None, :] * inv[d % 32][:, None], col[None, :] * inv[d % 32][:, None]).astype(np.float32)
    cos = np.cos(ang).astype(np.float32)
    sin = np.sin(ang).astype(np.float32)
    cos[:, ~valid] = 1.0
    sin[:, ~valid] = 0.0
    return np.ascontiguousarray(np.stack([cos, sin], axis=1))


def _perm():
    Pm = np.zeros((128, 128), np.float32)
    for base in (0, 64):
        for i in range(32):
            Pm[base + 32 + i, base + i] = -1.0
            Pm[base + i, base + 32 + i] = 1.0
    return Pm


def _fm(v, nchunk):
    return np.ascontiguousarray(np.asarray(v, np.float32).reshape(nchunk, 128).T)


_NC_CACHE = {}


def _get_nc(mode, dump=False):
    key = (mode, dump)
    if key not in _NC_CACHE:
        _NC_CACHE[key] = Builder(mode, dump).build()
    return _NC_CACHE[key]


def _common_inputs(inp):
    f = lambda a: np.ascontiguousarray(np.asarray(a, np.float32))
    norm_gT = np.stack([_fm(inp["norm_g"][l], 16) for l in range(4)], axis=1)
    ada_bT = np.stack([_fm(inp["ada_b"][l], 48) for l in range(4)], axis=1)
    ada_bT = np.ascontiguousarray(np.repeat(ada_bT[:, :, :, None], 2, axis=3))
    pool_scaleT = np.stack([_fm(inp["pool_scale"][j], 32) for j in range(2)], axis=1)
    ln_gbT = np.stack([_fm(inp["gmlp_ln_g"][0], 32), _fm(inp["gmlp_ln_b"][0], 32)], axis=1)
    w_sT = np.ascontiguousarray(np.asarray(inp["gmlp_w_s"][0], np.float32).transpose(2, 0, 1))
    b_s_row = f(inp["gmlp_b_s"][0]).reshape(1, 1024)
    sub = np.asarray(inp["diff_subln_g"][0], np.float32)
    dvec = np.stack([inp["diff_q_norm_g"][0], inp["diff_k_norm_g"][0], inp["diff_lq1"][0], inp["diff_lk1"][0],
                     inp["diff_lq2"][0], inp["diff_lk2"][0], sub[0:128], sub[128:256]], axis=1).astype(np.float32)
    return dict(
        ident=np.eye(128, dtype=np.float32), permT=_perm(),
        norm_gT=np.ascontiguousarray(norm_gT), ada_bT=ada_bT, ada_w=f(inp["ada_w"]),
        pool_w_in=f(inp["pool_w_in"]), pool_w_grp=f(inp["pool_w_grp"]), pool_scaleT=np.ascontiguousarray(pool_scaleT),
        pool_w_out=f(inp["pool_w_out"]), gmlp_w_in=f(inp["gmlp_w_in"][0]), ln_gbT=np.ascontiguousarray(ln_gbT),
        w_sT=w_sT, b_s_row=b_s_row, gmlp_w_out=f(inp["gmlp_w_out"][0]), diff_w_in=f(inp["diff_w_in"][0]),
        dvec=np.ascontiguousarray(dvec), diff_w_out=f(inp["diff_w_out"][0]),
    )


def _core_inputs(inp, core, which):
    b, s = core // 2, core % 2
    x_b = np.asarray(inp["x"][b], np.float32)
    ctx_b = np.asarray(inp["ctx"][b], np.float32)
    d = {}
    for tag, ss in which:
        hm, ec = _masks(ss)
        d["x" + tag] = _cols(x_b, ctx_b, ss)
        d["hmask_" + tag] = hm
        d["ecorr_" + tag] = ec
        d["rope_" + tag] = _rope(ss)
    cv = np.stack([_fm(inp["c"][b], 16), _fm(inp["c_ctx"], 16)], axis=2)
    d["cvec"] = np.ascontiguousarray(cv)
    return d


def _assemble(results):
    out = np.zeros((4, 2048, D), np.float32)
    for core in range(8):
        b, s = core // 2, core % 2
        o = results[core]["out"]
        out[b, s * 1024:(s + 1) * 1024, :] = o.transpose(2, 1, 0).reshape(1024, D)
    return out


MODE = "fused"


def kernel(**inp):
    inp = {k: np.asarray(v) for k, v in inp.items()}
    common = _common_inputs(inp)
    if MODE == "fused":
        nc = _get_nc("fused")
        in_maps = []
        for core in range(8):
            s = core % 2
            d = dict(common)
            d.update(_core_inputs(inp, core, [("a", 1 - s), ("b", s)]))
            in_maps.append(d)
        res = run_bass_kernel_spmd(nc, in_maps, core_ids=list(range(8)))
        return _assemble(res.results)
    raise NotImplementedError
```

```python
import math
from contextlib import ExitStack

import numpy as np
import concourse.bass as bass
import concourse.mybir as mybir
from concourse.bass_utils import run_bass_kernel_spmd

F32 = mybir.dt.float32
BF16 = mybir.dt.bfloat16
AF = mybir.ActivationFunctionType
ALU = mybir.AluOpType

D = 2048
KC = 16
NT = 1184
NIN = 1152
EPS = 1e-6
ENGINES = ["tensor", "vector", "scalar", "gpsimd", "sync"]
PIECES = [(0, 512), (512, 1024), (1024, 1184)]
INNER = [(0, 512), (512, 1024), (1024, 1152)]
WSLOT = 4096
NSLOT = 3
import os as _os_dbg
NO_ADA_BG = bool(_os_dbg.environ.get("NO_ADA_BG"))


def piece_ids(c0, c1):
    return [i for i, (a, b) in enumerate(PIECES) if a < c1 and c0 < b]


class Prog:
    def __init__(self, nc, ctx):
        self.nc = nc
        self.ctx = ctx
        self.q = {e: [] for e in ENGINES}
        self.sems = {}
        self.cnt = {}
        for e in ENGINES:
            self._mksem("e_" + e)
        self.waited = {e: {} for e in ENGINES}
        self.res = {}

    def _mksem(self, name):
        if name not in self.sems:
            self.sems[name] = self.ctx.enter_context(self.nc.semaphore(name))
            self.cnt[name] = 0
        return self.sems[name]

    def _deps(self, reads, writes):
        d = {}

        def add(s, v):
            if d.get(s, 0) < v:
                d[s] = v

        for k in reads:
            r = self.res.get(k)
            if r is not None and r[0] is not None:
                add(*r[0])
        for k in writes:
            r = self.res.get(k)
            if r is not None:
                if r[0] is not None:
                    add(*r[0])
                for s, v in r[1].items():
                    add(s, v)
        return d

    def _commit(self, ev, reads, writes):
        s, v = ev
        for k in reads:
            r = self.res.get(k)
            if r is None:
                r = self.res[k] = [None, {}]
            if r[1].get(s, 0) < v:
                r[1][s] = v
        for k in writes:
            self.res[k] = [ev, {}]

    def _waits(self, eng, deps):
        w = []
        wd = self.waited[eng]
        for s, v in deps.items():
            if wd.get(s, 0) < v:
                wd[s] = v
                w.append((self.sems[s], v))
        return w

    def op(self, eng, fn, reads=(), writes=()):
        deps = self._deps(reads, writes)
        waits = self._waits(eng, deps)
        sname = "e_" + eng
        self.cnt[sname] += 1
        ev = (sname, self.cnt[sname])
        sem = self.sems[sname]

        def emit(e, waits=waits, fn=fn, sem=sem):
            for s, v in waits:
                e.wait_ge(s, v)
            fn(e).then_inc(sem, 1)

        self.q[eng].append(emit)
        self._commit(ev, reads, writes)
        return ev

    def dma(self, queue, dsem, out, in_, reads=(), writes=()):
        deps = self._deps(reads, writes)
        waits = self._waits(queue, deps)
        sem = self._mksem(dsem)
        self.cnt[dsem] += 16
        ev = (dsem, self.cnt[dsem])

        def emit(e, waits=waits, sem=sem, out=out, in_=in_):
            for s, v in waits:
                e.wait_ge(s, v)
            e.dma_start(out=out, in_=in_).then_inc(sem, 16)

        self.q[queue].append(emit)
        self._commit(ev, reads, writes)
        return ev

    def wait_all(self, eng, keys):
        deps = self._deps((), keys)
        waits = self._waits(eng, deps)

        def emit(e, waits=waits):
            for s, v in waits:
                e.wait_ge(s, v)

        self.q[eng].append(emit)

    def barrier(self, engines=("tensor", "vector", "scalar", "sync")):
        snap = {s: v for s, v in self.cnt.items() if v > 0}
        for eng in engines:
            waits = self._waits(eng, dict(snap))

            def emit(e, waits=waits):
                for s, v in waits:
                    e.wait_ge(s, v)

            self.q[eng].append(emit)

    def emit_all(self):
        with self.nc.Block() as block:
            @block.tensor
            def _(e):
                for f in self.q["tensor"]:
                    f(e)

            @block.vector
            def _(e):
                for f in self.q["vector"]:
                    f(e)

            @block.scalar
            def _(e):
                for f in self.q["scalar"]:
                    f(e)

            @block.gpsimd
            def _(e):
                for f in self.q["gpsimd"]:
                    f(e)

            @block.sync
            def _(e):
                for f in self.q["sync"]:
                    f(e)


class Builder:
    def __init__(self, mode, dump=False, lite=False, stop=None):
        self.lite = lite
        self.stop = stop
        self.mode = mode
        self.dump = dump
        self.nc = bass.Bass("TRN2", target_bir_lowering=False)
        self.dram = {}
        self.ps_rr = 0
        self.st_rr = 0
        self.w_i = 0
        self.ada_done = set()
        self.ada_queue = []
        self.ada_inflight = None
        self.ada_cur = None
        self.okeys = []

    def dump_pt(self, nm):
        if nm in self.dump_t:
            self.P.dma("sync", "d_" + nm, self.dump_t[nm][:, :, :], self.xT[:],
                       reads=[("x", k, p) for k in range(KC) for p in range(3)], writes=[self.okey()])

    def okey(self):
        k = ("okey", len(self.okeys))
        self.okeys.append(k)
        return k

    def din(self, name, shape, dt=F32):
        t = self.nc.dram_tensor(name, list(shape), dt, kind="ExternalInput").ap()
        self.dram[name] = t
        return t

    def dout(self, name, shape, dt=F32):
        t = self.nc.dram_tensor(name, list(shape), dt, kind="ExternalOutput").ap()
        self.dram[name] = t
        return t

    def dint(self, name, shape, dt=F32):
        t = self.nc.dram_tensor(name, list(shape), dt, kind="Internal").ap()
        self.dram[name] = t
        return t

    def act(self, out, in_, func, reads, writes, **kw):
        return self.P.op("scalar", lambda e: e.activation(out=out, in_=in_, func=func, **kw), reads, writes)

    def tt(self, out, in0, in1, op, reads, writes):
        return self.P.op("vector", lambda e: e.tensor_tensor(out=out, in0=in0, in1=in1, op=op), reads, writes)

    def ts(self, out, in0, s1, s2, op0, op1, reads, writes):
        if s2 is None:
            return self.P.op("vector", lambda e: e.tensor_scalar(out=out, in0=in0, scalar1=s1, scalar2=None, op0=op0),
                             reads, writes)
        return self.P.op("vector", lambda e: e.tensor_scalar(out=out, in0=in0, scalar1=s1, scalar2=s2, op0=op0, op1=op1),
                         reads, writes)

    def stt(self, out, in0, scalar, in1, op0, op1, reads, writes):
        return self.P.op("vector", lambda e: e.scalar_tensor_tensor(out=out, in0=in0, scalar=scalar, in1=in1, op0=op0, op1=op1),
                         reads, writes)

    def vcopy(self, out, in_, reads, writes):
        return self.P.op("vector", lambda e: e.tensor_copy(out=out, in_=in_), reads, writes)

    def recip(self, out, in_, reads, writes):
        return self.P.op("vector", lambda e: e.reciprocal(out=out, in_=in_), reads, writes)

    def pe(self, mms, reads, writes):
        def fn(e, mms=mms):
            ins = None
            for (o, l, r, st, sp) in mms:
                ins = e.matmul(o, l, r, start=st, stop=sp)
            return ins
        return self.P.op("tensor", fn, reads, writes)

    def pet(self, tps, reads, writes):
        def fn(e, tps=tps):
            ins = None
            for (o, i, idn) in tps:
                ins = e.transpose(o, i, idn)
            return ins
        return self.P.op("tensor", fn, reads, writes)

    def bank(self, b, dt=F32):
        a = self.ps[:, b * 512:(b + 1) * 512]
        return a if dt == F32 else a.bitcast(dt)

    def wload(self, w2d, kc, c0, ncols):
        r = self.wload_raw(w2d, kc, c0, ncols)
        self.ada_pump()
        return r

    def wload_raw(self, w2d, kc, c0, ncols):
        s = self.w_i % NSLOT
        self.w_i += 1
        assert kc * ncols <= WSLOT
        view = self.wslots[:, s, 0:kc * ncols].rearrange("p (k n) -> p k n", n=ncols)
        src = w2d.rearrange("(k p) n -> p k n", p=128)[:, :, c0:c0 + ncols]
        self.P.dma("gpsimd", "w%d" % s, view, src, writes=[("w", s)])
        return view, ("w", s)

    def arena_reset(self):
        self.a_off = 0

    def carve(self, shape, dt):
        n = 1
        for d_ in shape[1:]:
            n *= d_
        words = n if dt == F32 else (n + 1) // 2
        words = (words + 15) // 16 * 16
        o = self.a_off
        self.a_off += words
        assert self.a_off <= self.ARENA, (self.a_off, self.ARENA)
        v = self.arena[:, o:o + words]
        if dt != F32:
            v = v.bitcast(dt)
        v = v[:, 0:n]
        if len(shape) == 2:
            return v
        if len(shape) == 3:
            return v.rearrange("p (a b) -> p a b", b=shape[2])
        return v.rearrange("p (a b c) -> p a b c", b=shape[2], c=shape[3])

    def build(self):
        nc = self.nc
        mode = self.mode
        fused = mode == "fused"
        if mode in ("fused",):
            xa = self.din("xa", [128, KC, NT])
            hmask_a = self.din("hmask_a", [128, 32])
            ecorr_a = self.din("ecorr_a", [128, 128])
            rope_a = self.din("rope_a", [128, 2, NT])
        xb = self.din("xb", [128, KC, NT])
        hmask_b = self.din("hmask_b", [128, 32])
        ecorr_b = self.din("ecorr_b", [128, 128])
        rope_b = self.din("rope_b", [128, 2, NT])
        cvec = self.din("cvec", [128, KC, 2])
        ident = self.din("ident", [128, 128])
        permT = self.din("permT", [128, 128])
        norm_gT = self.din("norm_gT", [128, 4, KC])
        ada_bT = self.din("ada_bT", [128, 4, 48, 2])
        if self.lite:
            ada_w = self.din("ada_w", [4, 128, 128])
            pool_w_in = self.din("pool_w_in", [2, 128, 128])
            pool_w_grp = self.din("pool_w_grp", [2, 4, 128, 128])
            pool_w_out = self.din("pool_w_out", [2, 128, 128])
            gmlp_w_in = self.din("gmlp_w_in", [128, 128])
        else:
            ada_w = self.din("ada_w", [4, D, 3 * D])
            pool_w_in = self.din("pool_w_in", [2, D, 8192])
            pool_w_grp = self.din("pool_w_grp", [2, 4, 1024, 1024])
            pool_w_out = self.din("pool_w_out", [2, 4096, D])
            gmlp_w_in = self.din("gmlp_w_in", [D, 12288])
        pool_scaleT = self.din("pool_scaleT", [128, 2, 32])
        ln_gbT = self.din("ln_gbT", [128, 2, 32])
        w_sT = self.din("w_sT", [128, 8, 128])
        b_s_row = self.din("b_s_row", [1, 1024])
        gmlp_w_out = self.din("gmlp_w_out", [128, 128] if self.lite else [4096, D])
        diff_w_in = self.din("diff_w_in", [D, 8192])
        dvec = self.din("dvec", [128, 8])
        diff_w_out = self.din("diff_w_out", [D, D])
        out = self.dout("out", [128, KC, 1024])
        self.dump_t = {}
        if self.dump:
            xdump = self.dout("xdump", [128, KC, NT])
            for nm in ("dA1", "dB0", "dB1", "dB2"):
                self.dump_t[nm] = self.dout(nm, [128, KC, NT])
        if mode == "u1":
            k_oth = self.dout("k_oth", [D, NIN], BF16)
            v_oth = self.dout("v_oth", [NIN, D], BF16)
        elif mode == "u2":
            k_oth = self.din("k_oth", [D, NIN], BF16)
            v_oth = self.din("v_oth", [NIN, D], BF16)
        else:
            k_oth = self.dint("k_oth", [D, NIN], BF16)
            v_oth = self.dint("v_oth", [NIN, D], BF16)
        k_own = self.dint("k_own", [D, NIN], BF16)
        v_own = self.dint("v_own", [NIN, D], BF16)
        q_own = self.dint("q_own", [D, NT], BF16)
        sg_own = self.dint("sg_own", [D, NT], BF16)

        with ExitStack() as ctx:
            E = ctx.enter_context
            self.xT = E(nc.sbuf_tensor("xT", [128, KC, NT], F32))
            self.wslots = E(nc.sbuf_tensor("wslots", [128, NSLOT, WSLOT], BF16))
            self.ident_f = E(nc.sbuf_tensor("ident_f", [128, 128], F32))
            self.ident_b = E(nc.sbuf_tensor("ident_b", [128, 128], BF16))
            self.ones_b = E(nc.sbuf_tensor("ones_b", [128, 128], BF16))
            self.ones_f = E(nc.sbuf_tensor("ones_f", [128, 128], F32))
            self.perm_f = E(nc.sbuf_tensor("perm_f", [128, 128], F32))
            self.perm_b = E(nc.sbuf_tensor("perm_b", [128, 128], BF16))
            self.cv = E(nc.sbuf_tensor("cv", [128, KC, 2], F32))
            self.scv = E(nc.sbuf_tensor("scv", [128, KC, 2], BF16))
            self.modall = E(nc.sbuf_tensor("modall", [128, 4, 48, 2], F32))
            self.modA = E(nc.sbuf_tensor("modA", [128, 4, KC, 2], F32))
            self.ngT = E(nc.sbuf_tensor("ngT", [128, 4, KC], F32))
            self.abT = E(nc.sbuf_tensor("abT", [128, 4, 48, 2], F32))
            self.pscT = E(nc.sbuf_tensor("pscT", [128, 2, 32], F32))
            self.lngb = E(nc.sbuf_tensor("lngb", [128, 2, 32], F32))
            self.dv = E(nc.sbuf_tensor("dv", [128, 8], F32))
            self.dsm = E(nc.sbuf_tensor("dsm", [128, 16], F32))
            self.adarow = E(nc.sbuf_tensor("adarow", [2, 256], F32))
            self.xsave = E(nc.sbuf_tensor("xsave", [128, KC, 16], F32))
            self.hmask = E(nc.sbuf_tensor("hmask", [128, 32], F32))
            self.ecorr = E(nc.sbuf_tensor("ecorr", [128, 128], F32))
            self.ARENA = 25472
            self.arena = E(nc.sbuf_tensor("arena", [128, self.ARENA], F32))
            self.ps = E(nc.psum_tensor("ps", [128, 4096], F32))
            self.P = P = Prog(nc, ctx)

            P.dma("sync", "d_c0", self.ident_f[:], ident[:, :], writes=["ident_f"])
            P.dma("sync", "d_c1", self.perm_f[:], permT[:, :], writes=["perm_f"])
            P.dma("sync", "d_c2", self.cv[:], cvec[:, :, :], writes=["cv"])
            P.dma("sync", "d_c3", self.ngT[:], norm_gT[:, :, :], writes=["ngT"])
            P.dma("sync", "d_c4", self.abT[:], ada_bT[:, :, :, :], writes=["abT"])
            P.dma("sync", "d_c5", self.pscT[:], pool_scaleT[:, :, :], writes=["pscT"])
            P.dma("sync", "d_c6", self.lngb[:], ln_gbT[:, :, :], writes=["lngb"])
            P.dma("sync", "d_c7", self.dv[:], dvec[:, :], writes=["dv"])
            P.op("vector", lambda e: e.memset(self.ones_b[:], 1.0), writes=["ones_b"])
            P.op("vector", lambda e: e.memset(self.ones_f[:], 1.0), writes=["ones_f"])
            self.vcopy(self.ident_b[:], self.ident_f[:], ["ident_f"], ["ident_b"])
            self.vcopy(self.perm_b[:], self.perm_f[:], ["perm_f"], ["perm_b"])
            self.act(self.scv[:], self.cv[:], AF.Silu, ["cv"], ["scv"])

            self.w = dict(ada_w=ada_w, pool_w_in=pool_w_in, pool_w_grp=pool_w_grp, pool_w_out=pool_w_out,
                          gmlp_w_in=gmlp_w_in, gmlp_w_out=gmlp_w_out, diff_w_in=diff_w_in, diff_w_out=diff_w_out,
                          w_sT=w_sT, b_s_row=b_s_row)
            self.kv = dict(k_oth=k_oth, v_oth=v_oth, k_own=k_own, v_own=v_own, q_own=q_own, sg_own=sg_own)

            def load_pass(xin, hm, ec):
                for k4 in range(4):
                    P.dma("sync", "d_x%d" % k4, self.xT[:, k4 * 4:(k4 + 1) * 4, :], xin[:, k4 * 4:(k4 + 1) * 4, :],
                          writes=[("x", k, p) for k in range(k4 * 4, k4 * 4 + 4) for p in range(3)])
                P.dma("sync", "d_hm", self.hmask[:], hm[:, :], writes=["hmask"])
                P.dma("sync", "d_ec", self.ecorr[:], ec[:, :], writes=["ecorr"])

            if mode == "fused":
                load_pass(xa, hmask_a, ecorr_a)
                self.pool_layer(0, 0)
                self.gmlp_layer(1)
                self.dump_pt("dA1")
                self.vcopy(self.xsave[:, :, 0:8], self.xT[:, :, 1016:1024],
                           [("x", k, 1) for k in range(KC)], ["xsave"])
                self.vcopy(self.xsave[:, :, 8:16], self.xT[:, :, 0:8],
                           [("x", k, 0) for k in range(KC)], ["xsave"])
                self.diff_layer(2, rope_a, kv_only=True, kdst=k_oth, vdst=v_oth)
                P.barrier()
                load_pass(xb, hmask_b, ecorr_b)
                self.pool_layer(0, 0)
                self.dump_pt("dB0")
                self.gmlp_layer(1)
                self.dump_pt("dB1")
                self.vcopy(self.xT[:, :, 1168:1184], self.xsave[:, :, :],
                           ["xsave"], [("x", k, 2) for k in range(KC)])
                self.diff_layer(2, rope_b, kv_only=False, kdst=k_own, vdst=v_own)
                self.dump_pt("dB2")
                self.pool_layer(3, 1)
            elif mode == "u1":
                load_pass(xb, hmask_b, ecorr_b)
                self.pool_layer(0, 0)
                self.gmlp_layer(1)
                self.diff_layer(2, rope_b, kv_only=True, kdst=k_oth, vdst=v_oth)
            elif mode == "u2":
                load_pass(xb, hmask_b, ecorr_b)
                self.diff_layer(2, rope_b, kv_only=False, kdst=k_own, vdst=v_own)
                self.pool_layer(3, 1)
            elif mode == "d0":
                load_pass(xb, hmask_b, ecorr_b)
                self.pool_layer(0, 0)
            elif mode == "d1":
                load_pass(xb, hmask_b, ecorr_b)
                self.gmlp_layer(1)
            elif mode == "d2":
                load_pass(xb, hmask_b, ecorr_b)
                self.diff_layer(2, rope_b, kv_only=False, kdst=k_own, vdst=v_own)

            allx = [("x", k, p) for k in range(KC) for p in range(3)]
            for k4 in range(4):
                P.dma("sync", "d_o%d" % k4, out[:, k4 * 4:(k4 + 1) * 4, :], self.xT[:, k4 * 4:(k4 + 1) * 4, 0:1024],
                      reads=allx, writes=[("out", k4)])
            if self.dump:
                P.dma("sync", "d_dump", xdump[:, :, :], self.xT[:], reads=allx, writes=[("out", 9)])
            P.wait_all("sync", [("out", i) for i in range(4)] + [("out", 9)] + self.okeys)
            P.emit_all()
        return nc

    def ada_block_load(self, l, blk):
        wv, wk = self.wload_raw(self.w["ada_w"][l], KC, blk * 256, 256)
        return (l, blk, wv, wk)

    def ada_block_compute(self, item):
        l, blk, wv, wk = item
        g = self.bank(6)
        self.pe([(g[0:2, 0:256], self.scv[:, k, :], wv[:, k, :], k == 0, k == KC - 1) for k in range(KC)],
                [wk, "scv"], [("ps", 6)])
        self.act(self.adarow[0:2, :], g[0:2, 0:256], AF.Identity, [("ps", 6)], ["adarow"])
        t = self.bank(7)
        self.pe([(t[:, c * 2:c * 2 + 2], self.adarow[0:2, c * 128:(c + 1) * 128], self.ident_f[0:2, 0:2], True, True)
                 for c in range(2)], ["adarow", "ident_f"], [("ps", 7)])
        self.tt(self.modall[:, l, blk * 2:blk * 2 + 2, :], t[:, 0:4].rearrange("p (a b) -> p a b", b=2),
                self.abT[:, l, blk * 2:blk * 2 + 2, :], ALU.add, [("ps", 7), "abT"], [("mod", l)])

    def ada_finish(self, l):
        for kind in range(2):
            self.stt(self.modA[:, l, :, kind], self.modall[:, l, 16:32, kind], 1.0, self.ngT[:, l, :],
                     ALU.add, ALU.mult, [("mod", l), "ngT"], [("modA", l)])

    def ada_pump(self):
        if self.ada_inflight is not None:
            self.ada_block_compute(self.ada_inflight)
            self.ada_inflight = None
            if not self.ada_queue:
                self.ada_finish(self.ada_cur)
                self.ada_done.add(self.ada_cur)
                self.ada_cur = None
        if self.ada_queue:
            l, blk = self.ada_queue.pop(0)
            self.ada_inflight = self.ada_block_load(l, blk)

    def ada_schedule(self, l):
        if NO_ADA_BG or self.lite or l in self.ada_done or self.ada_cur == l or l > 3:
            return
        self.ada_cur = l
        self.ada_queue = [(l, blk) for blk in range(24)]

    def ada_layer(self, l):
        if l in self.ada_done:
            return
        P = self.P
        if self.lite:
            self.ada_done.add(l)
            P.op("vector", lambda e: e.memset(self.modall[:, l, :, :], 0.1), [], [("mod", l)])
            P.op("vector", lambda e: e.memset(self.modA[:, l, :, :], 1.0), [], [("modA", l)])
            return
        if self.ada_cur == l:
            while self.ada_cur == l:
                self.ada_pump()
            return
        self.ada_done.add(l)
        for blk in range(24):
            self.ada_block_compute(self.ada_block_load(l, blk))
        self.ada_finish(l)

    def emit_hT(self, l, ranges, hT, sq, rt, rstd, tmp):
        for ri, (c0, c1, kind, d0) in enumerate(ranges):
            n = c1 - c0
            pcs = piece_ids(c0, c1)
            sb = 6 + (ri % 2)
            g = self.bank(sb)
            for kg in range(4):
                s_ = sq[kg % 2]
                self.act(s_[:, :, 0:n], self.xT[:, kg * 4:(kg + 1) * 4, c0:c1], AF.Square,
                         [("x", k, p) for k in range(kg * 4, kg * 4 + 4) for p in pcs], [("sq", kg % 2)])
                self.pe([(g[:, 0:n], self.ones_b[:], s_[:, q, 0:n], kg == 0 and q == 0, kg == 3 and q == 3)
                         for q in range(4)], [("sq", kg % 2), "ones_b"], [("ps", sb)])
            r_ = rt[ri % 2]
            rs_ = rstd[ri % 2]
            self.act(r_[:, 0:n], g[:, 0:n], AF.Sqrt, [("ps", sb)], [("rt", ri % 2)], scale=1.0 / D, bias=EPS)
            self.recip(rs_[:, 0:n], r_[:, 0:n], [("rt", ri % 2)], [("rstd", ri % 2)])
            for k in range(KC):
                t_ = tmp[k % 2]
                self.stt(t_[:, 0:n], self.xT[:, k, c0:c1], self.modA[:, l, k, kind:kind + 1], rs_[:, 0:n],
                         ALU.mult, ALU.mult, [("x", k, p) for p in pcs] + [("rstd", ri % 2), ("modA", l)],
                         [("tmp", k % 2)])
                self.act(hT[:, k, d0:d0 + n], t_[:, 0:n], AF.Identity, [("tmp", k % 2), ("mod", l)], ["hT"],
                         bias=self.modall[:, l, k, kind:kind + 1], scale=1.0)

    def hT_scratch(self):
        sq = [self.carve([128, 4, 512], BF16) for _ in range(2)]
        rt = [self.carve([128, 512], F32) for _ in range(2)]
        rstd = [self.carve([128, 512], F32) for _ in range(2)]
        tmp = [self.carve([128, 512], F32) for _ in range(2)]
        return sq, rt, rstd, tmp

    def next_set(self):
        b0 = 3 * (self.ps_rr % 2)
        self.ps_rr += 1
        return b0

    def mm_cols(self, b0, pieces, wv, cc, rhs_of, nk):
        mms = []
        for k in range(nk):
            for pi, (a, b) in enumerate(pieces):
                mms.append((self.bank(b0 + pi)[:, 0:b - a], wv[:, k, cc * 128:(cc + 1) * 128], rhs_of(k, a, b),
                            k == 0, k == nk - 1))
        return mms

    def residual(self, l, mc, b0, pieces_kind):
        for pi, (a, b, kind) in enumerate(pieces_kind):
            pcs = piece_ids(a, b)
            self.stt(self.xT[:, mc, a:b], self.bank(b0 + pi)[:, 0:b - a], self.modall[:, l, 32 + mc, kind:kind + 1],
                     self.xT[:, mc, a:b], ALU.mult, ALU.add,
                     [("ps", b0 + pi), ("mod", l)] + [("x", mc, p) for p in pcs], [("x", mc, p) for p in pcs])

    def pool_layer(self, l, j):
        P = self.P
        self.ada_layer(l)
        self.ada_schedule(l + 1)
        P.barrier()
        self.arena_reset()
        hT = self.carve([128, KC, NT], BF16)
        pbuf = self.carve([128, 8, NIN], BF16)
        sgb = self.carve([128, 8, NIN], BF16)
        mark = self.a_off
        sq, rt, rstd, tmp = self.hT_scratch()
        self.emit_hT(l, [(0, 512, 0, 0), (512, 1024, 0, 512), (1024, 1168, 1, 1024), (1168, 1184, 0, 1168)],
                     hT, sq, rt, rstd, tmp)
        P.barrier()
        self.a_off = mark
        zext = [self.carve([128, NT], F32) for _ in range(2)]
        abuf = [self.carve([128, NT], F32) for _ in range(2)]
        w_in = self.w["pool_w_in"][j]
        w_out = self.w["pool_w_out"][j]
        hm = self.hmask
        ec = self.ecorr[:].rearrange("p (w e t) -> p w e t", e=4, t=8)
        inner = INNER if l != 3 else INNER[:2]
        for grp in range(4):
            win = 2 ** (grp + 1)
            hw = win // 2
            for zb in range(4):
                wv, wk = self.wload(w_in, KC, (grp * 8 + zb * 2) * 128, 256)
                for cc in range(2):
                    zc = zb * 2 + cc
                    b0 = self.next_set()
                    self.pe(self.mm_cols(b0, PIECES, wv, cc, lambda k, a, b: hT[:, k, a:b], KC),
                            [wk, "hT"], [("ps", b0), ("ps", b0 + 1), ("ps", b0 + 2)])
                    zi = zc % 2
                    zx = zext[zi]
                    zk = ("zx", zi)
                    b2 = self.bank(b0 + 2)
                    self.act(zx[:, 8:520], self.bank(b0)[:, 0:512], AF.Identity, [("ps", b0)], [zk])
                    self.act(zx[:, 520:1032], self.bank(b0 + 1)[:, 0:512], AF.Identity, [("ps", b0 + 1)], [zk])
                    self.act(zx[:, 1048:1176], b2[:, 0:128], AF.Identity, [("ps", b0 + 2)], [zk])
                    for (dst, src, mo) in ((1040, 128, 0), (1176, 136, 8), (0, 144, 16), (1032, 152, 24)):
                        self.act(zx[:, dst:dst + 8], b2[:, src:src + 8], AF.Identity, [("ps", b0 + 2), "hmask"], [zk],
                                 scale=hm[:, mo:mo + 1])
                    a_, b_ = abuf
                    self.tt(a_[:, 0:1183], zx[:, 0:1183], zx[:, 1:1184], ALU.add, [zk], ["ab0"])
                    fb, fk = a_, "ab0"
                    if win >= 4:
                        self.tt(b_[:, 0:1181], a_[:, 0:1181], a_[:, 2:1183], ALU.add, ["ab0"], ["ab1"])
                        fb, fk = b_, "ab1"
                    if win >= 8:
                        self.tt(a_[:, 0:1177], b_[:, 0:1177], b_[:, 4:1181], ALU.add, ["ab1"], ["ab0"])
                        fb, fk = a_, "ab0"
                    if win >= 16:
                        self.tt(b_[:, 0:1169], a_[:, 0:1169], a_[:, 8:1177], ALU.add, ["ab0"], ["ab1"])
                        fb, fk = b_, "ab1"
                    for ei, base in enumerate((8 - hw, 8 - hw + 1016, 1048 - hw, 1048 - hw + 120)):
                        self.tt(fb[:, base:base + 8], fb[:, base:base + 8], ec[:, grp, ei, :], ALU.mult,
                                [fk, "ecorr"], [fk])
                    self.stt(pbuf[:, zc, 0:1024], fb[:, 8 - hw:8 - hw + 1024], 1.0 / win, zx[:, 8:1032],
                             ALU.mult, ALU.subtract, [fk, zk], [("p", zc)])
                    self.stt(pbuf[:, zc, 1024:1152], fb[:, 1048 - hw:1048 - hw + 128], 1.0 / win, zx[:, 1048:1176],
                             ALU.mult, ALU.subtract, [fk, zk], [("p", zc)])
            for gb in range(4):
                wv, wk = self.wload(w_in, KC, 4096 + (grp * 8 + gb * 2) * 128, 256)
                for cc in range(2):
                    gc = gb * 2 + cc
                    b0 = self.next_set()
                    self.pe(self.mm_cols(b0, inner, wv, cc, lambda k, a, b: hT[:, k, a:b], KC),
                            [wk, "hT"], [("ps", b0), ("ps", b0 + 1), ("ps", b0 + 2)])
                    for pi, (a, b) in enumerate(inner):
                        self.act(sgb[:, gc, a:b], self.bank(b0 + pi)[:, 0:b - a], AF.Silu, [("ps", b0 + pi)], [("sg", gc)])
            for nb in range(2):
                wv, wk = self.wload(self.w["pool_w_grp"][j][grp], 8, nb * 512, 512)
                for cc in range(4):
                    mc = nb * 4 + cc
                    b0 = self.next_set()
                    self.pe(self.mm_cols(b0, inner, wv, cc, lambda k, a, b: pbuf[:, k, a:b], 8),
                            [wk] + [("p", k) for k in range(8)], [("ps", b0), ("ps", b0 + 1), ("ps", b0 + 2)])
                    for pi, (a, b) in enumerate(inner):
                        self.stt(sgb[:, mc, a:b], self.bank(b0 + pi)[:, 0:b - a], self.pscT[:, j, grp * 8 + mc:grp * 8 + mc + 1],
                                 sgb[:, mc, a:b], ALU.mult, ALU.mult, [("ps", b0 + pi), ("sg", mc), "pscT"], [("sg", mc)])
            wo = w_out[grp * 1024:(grp + 1) * 1024, :]
            for nb in range(4):
                wv, wk = self.wload(wo, 8, nb * 512, 512)
                for cc in range(4):
                    mc = nb * 4 + cc
                    b0 = self.next_set()
                    self.pe(self.mm_cols(b0, inner, wv, cc, lambda k, a, b: sgb[:, k, a:b], 8),
                            [wk] + [("sg", k) for k in range(8)], [("ps", b0), ("ps", b0 + 1), ("ps", b0 + 2)])
                    self.residual(l, mc, b0, [(0, 512, 0), (512, 1024, 0), (1024, 1152, 1)][:len(inner)])

    def gmlp_layer(self, l):
        P = self.P
        self.ada_layer(l)
        self.ada_schedule(l + 1)
        P.barrier()
        self.arena_reset()
        wsf2 = self.carve([128, 1024], F32)
        wsb2 = self.carve([128, 1024], BF16)
        Rb2 = self.carve([128, 1024], F32)
        bsb2 = self.carve([128, 1024], F32)
        bsrow = self.carve([128, 1024], F32)
        wsb = wsb2.rearrange("p (a b) -> p a b", b=128)
        Rb = Rb2.rearrange("p (a b) -> p a b", b=128)
        bsb = bsb2.rearrange("p (a b) -> p a b", b=128)
        P.dma("sync", "d_g0", wsf2, self.w["w_sT"].rearrange("p a b -> p (a b)"), writes=["wsf"])
        P.dma("sync", "d_g1", bsrow[0:1, :], self.w["b_s_row"][:, :], writes=["bsrow"])
        self.vcopy(wsb2, wsf2, ["wsf"], ["wsb"])
        for half in range(2):
            g = self.bank(6 + half)
            self.pe([(g[:, 0:512], self.ones_b[:], wsb2[:, half * 512:(half + 1) * 512], True, True)],
                    ["wsb", "ones_b"], [("ps", 6 + half)])
            self.vcopy(Rb2[:, half * 512:(half + 1) * 512], g[:, 0:512], [("ps", 6 + half)], ["Rb"])
        for half in range(2):
            g = self.bank(6 + half)
            self.pe([(g[:, 0:512], self.ones_f[0:1, :], bsrow[0:1, half * 512:(half + 1) * 512], True, True)],
                    ["bsrow", "ones_f"], [("ps", 6 + half)])
            self.vcopy(bsb2[:, half * 512:(half + 1) * 512], g[:, 0:512], [("ps", 6 + half)], ["bsb"])
        base = self.a_off
        w_in = self.w["gmlp_w_in"]
        w_out = self.w["gmlp_w_out"]
        lng = self.lngb[:, 0, :]
        lnb = self.lngb[:, 1, :]
        subpasses = [
            dict(ranges=[(0, 512, 0, 0), (1024, 1152, 1, 512)], nt=5),
            dict(ranges=[(512, 1024, 0, 0)], nt=4),
        ]
        for sp in subpasses:
            P.barrier()
            self.a_off = base
            nt = sp["nt"]
            ncol = nt * 128
            hs = self.carve([128, KC, ncol], BF16)
            vsm = self.carve([128, nt, 4096], BF16)
            junk = self.carve([128, 256], BF16)
            s1p = self.carve([128, nt, 16], F32)
            s2p = self.carve([128, nt, 16], F32)
            st = self.carve([128, nt, 8], F32)
            mark = self.a_off
            sq, rt, rstd, tmp = self.hT_scratch()
            self.emit_hT(l, sp["ranges"], hs, sq, rt, rstd, tmp)
            P.barrier()
            self.a_off = mark
            Bj = [self.carve([128, 128], F32) for _ in range(2)]
            ub = [self.carve([128, ncol], F32) for _ in range(2)]
            gbuf = [self.carve([128, ncol], F32) for _ in range(2)]
            cpieces = [(0, 512), (512, ncol)] if ncol > 512 else [(0, 512)]
            for vb in range(16):
                wv, wk = self.wload(w_in, KC, 4096 + vb * 256, 256)
                for ti in range(nt):
                    bk = self.st_rr % 4
                    self.st_rr += 1
                    g = self.bank(bk)
                    self.pe([(g[:, 0:256], hs[:, k, ti * 128:(ti + 1) * 128], wv[:, k, :], k == 0, k == KC - 1)
                             for k in range(KC)], [wk, "hT"], [("ps", bk)])
                    vk2 = [("v", ti, 2 * vb), ("v", ti, 2 * vb + 1)]
                    self.act(vsm[:, ti, vb * 256:(vb + 1) * 256], g[:, 0:256], AF.Gelu_apprx_tanh,
                             [("ps", bk)], vk2 + [("s1p", ti)], accum_out=s1p[:, ti, vb:vb + 1])
                    self.act(junk, vsm[:, ti, vb * 256:(vb + 1) * 256], AF.Square,
                             vk2, ["junk", ("s2p", ti)], accum_out=s2p[:, ti, vb:vb + 1])
            for ti in range(nt):
                sk = ("st", ti)
                self.P.op("vector", lambda e, o_=st[:, ti, 0:1], i_=s1p[:, ti, :]: e.reduce_sum(out=o_, in_=i_, axis=mybir.AxisListType.X),
                          [("s1p", ti)], [sk])
                self.P.op("vector", lambda e, o_=st[:, ti, 1:2], i_=s2p[:, ti, :]: e.reduce_sum(out=o_, in_=i_, axis=mybir.AxisListType.X),
                          [("s2p", ti)], [sk])
                self.ts(st[:, ti, 2:3], st[:, ti, 0:1], 1.0 / 4096, None, ALU.mult, None, [sk], [sk])
                self.tt(st[:, ti, 3:4], st[:, ti, 2:3], st[:, ti, 2:3], ALU.mult, [sk], [sk])
                self.stt(st[:, ti, 4:5], st[:, ti, 1:2], 1.0 / 4096, st[:, ti, 3:4], ALU.mult, ALU.subtract, [sk], [sk])
                self.act(st[:, ti, 5:6], st[:, ti, 4:5], AF.Sqrt, [sk], [sk], bias=EPS, scale=1.0)
                self.recip(st[:, ti, 6:7], st[:, ti, 5:6], [sk], [sk])
                vall = [("v", ti, jj) for jj in range(32)]
                self.ts(vsm[:, ti, :], vsm[:, ti, :], st[:, ti, 2:3], st[:, ti, 6:7], ALU.subtract, ALU.mult,
                        [sk] + vall, vall)
            def spatial(jc, Bj=None, vsm=None, nt=None):
                h = jc // 4
                bj = Bj[jc % 2]
                self.stt(bj, Rb[:, h, :], lnb[:, jc:jc + 1], bsb[:, h, :], ALU.mult, ALU.add,
                         ["Rb", "bsb", "lngb"], [("Bj", jc % 2)])
                bk = 6 + (jc % 2)
                bk2 = 2 + 3 * (jc % 2)
                mms = []
                for ti in range(nt):
                    o = self.bank(bk)[:, ti * 128:(ti + 1) * 128] if ti < 4 else self.bank(bk2)[:, 0:128]
                    mms.append((o, vsm[:, ti, jc * 128:(jc + 1) * 128], wsb[:, h, :], True, True))
                self.pe(mms, [("v", ti, jc) for ti in range(nt)] + ["wsb"], [("ps", bk), ("ps", bk2)])
                for ti in range(nt):
                    o = self.bank(bk)[:, ti * 128:(ti + 1) * 128] if ti < 4 else self.bank(bk2)[:, 0:128]
                    self.stt(vsm[:, ti, jc * 128:(jc + 1) * 128], o, lng[:, jc:jc + 1], bj, ALU.mult, ALU.add,
                             [("ps", bk), ("ps", bk2), ("Bj", jc % 2), "lngb"], [("v", ti, jc)])
            allv = [("v", ti, jj) for ti in range(nt) for jj in range(32)]
            for ub_i in range(16):
                wu, wuk = self.wload(w_in, KC, ub_i * 256, 256)
                for cc in range(2):
                    jc = ub_i * 2 + cc
                    b0 = self.next_set()
                    self.pe(self.mm_cols(b0, cpieces, wu, cc, lambda k, a, b: hs[:, k, a:b], KC),
                            [wuk, "hT"], [("ps", b0), ("ps", b0 + 1)])
                    for pi, (a, b) in enumerate(cpieces):
                        self.act(ub[cc][:, a:b], self.bank(b0 + pi)[:, 0:b - a], AF.Gelu_apprx_tanh,
                                 [("ps", b0 + pi)], [("ub", cc)])
                wg, wgk = self.wload(w_in, KC, 8192 + ub_i * 256, 256)
                for cc in range(2):
                    jc = ub_i * 2 + cc
                    b0 = self.next_set()
                    self.pe(self.mm_cols(b0, cpieces, wg, cc, lambda k, a, b: hs[:, k, a:b], KC),
                            [wgk, "hT"], [("ps", b0), ("ps", b0 + 1)])
                    for pi, (a, b) in enumerate(cpieces):
                        self.act(gbuf[cc][:, a:b], self.bank(b0 + pi)[:, 0:b - a], AF.Silu,
                                 [("ps", b0 + pi)], [("gb", cc)])
                    spatial(jc, Bj=Bj, vsm=vsm, nt=nt)
                    self.tt(gbuf[cc][:, 0:ncol], gbuf[cc][:, 0:ncol], ub[cc][:, 0:ncol], ALU.mult,
                            [("gb", cc), ("ub", cc)], [("gb", cc)])
                    self.tt(vsm[:, :, jc * 128:(jc + 1) * 128], vsm[:, :, jc * 128:(jc + 1) * 128],
                            gbuf[cc][:, 0:ncol].rearrange("p (t q) -> p t q", q=128), ALU.mult,
                            [("gb", cc)] + [("v", ti, jc) for ti in range(nt)], [("v", ti, jc) for ti in range(nt)])
            pk = []
            for (c0, c1, kind, d0) in sp["ranges"]:
                pk.append((c0, c1, kind, d0))
            for mc in range(KC):
                wv, wk = self.wload(w_out, 32, mc * 128, 128)
                b0 = self.next_set()
                mms = []
                for k in range(32):
                    for pi, (c0, c1, kind, d0) in enumerate(pk):
                        t0 = d0 // 128
                        ntile = (c1 - c0) // 128
                        mms.append((self.bank(b0 + pi)[:, 0:c1 - c0].rearrange("p (t q) -> p t q", q=128), wv[:, k, :],
                                    vsm[:, t0:t0 + ntile, k * 128:(k + 1) * 128], k == 0, k == 31))
                self.pe(mms, [wk] + allv, [("ps", b0 + pi) for pi in range(len(pk))])
                self.residual(l, mc, b0, [(c0, c1, kind) for (c0, c1, kind, d0) in pk])

    def diff_layer(self, l, rope_d, kv_only, kdst, vdst):
        P = self.P
        self.ada_layer(l)
        if not kv_only:
            self.ada_schedule(l + 1)
        P.barrier()
        self.arena_reset()
        lam_init = 0.8 - 0.6 * math.exp(-0.3 * l)
        dsm = self.dsm
        dv = self.dv
        self.tt(dsm[:, 0:1], dv[:, 2:3], dv[:, 3:4], ALU.mult, ["dv"], ["dsm"])
        self.tt(dsm[:, 1:2], dv[:, 4:5], dv[:, 5:6], ALU.mult, ["dv", "dsm"], ["dsm"])
        g = self.bank(6)
        self.pe([(g[:, 0:2], self.ones_f[:], dsm[:, 0:2], True, True)], ["dsm", "ones_f"], [("ps", 6)])
        self.act(dsm[:, 2:4], g[:, 0:2], AF.Exp, [("ps", 6), "dsm"], ["dsm"])
        self.tt(dsm[:, 4:5], dsm[:, 2:3], dsm[:, 3:4], ALU.subtract, ["dsm"], ["dsm"])
        self.ts(dsm[:, 5:6], dsm[:, 4:5], lam_init, -1.0, ALU.add, ALU.mult, ["dsm"], ["dsm"])
        self.ts(dsm[:, 6:8], dv[:, 6:8], 1.0 - lam_init, None, ALU.mult, None, ["dv", "dsm"], ["dsm"])
        neglam = dsm[:, 5:6]
        if self.stop == "LAM":
            return

        hT = self.carve([128, KC, NT], BF16)
        rope = self.carve([128, 2, NT], F32)
        P.dma("sync", "d_rope", rope, rope_d[:, :, :], writes=["rope"])
        sqb = [self.carve([128, 512], BF16) for _ in range(6)]
        qgb = [self.carve([128, 512], BF16) for _ in range(6)]
        rsb = [self.carve([128, 512], F32) for _ in range(3)]
        kst = [self.carve([128, NT], BF16) for _ in range(2)]
        vst = self.carve([128, 9, 256], BF16)
        mark = self.a_off
        sq, rt, rstd, tmp = self.hT_scratch()
        if kv_only:
            ranges = [(0, 512, 0, 0), (512, 1024, 0, 512), (1024, 1152, 1, 1024)]
        else:
            ranges = [(0, 512, 0, 0), (512, 1024, 0, 512), (1024, 1152, 1, 1024), (1168, 1184, 0, 1168)]
            P.op("vector", lambda e: e.memset(hT[:, :, 1152:1168], 0.0), [], ["hT"])
        self.emit_hT(l, ranges, hT, sq, rt, rstd, tmp)
        P.barrier()
        self.a_off = mark
        t1b = [self.carve([128, 512], F32) for _ in range(3)]
        t2b = [self.carve([128, 512], F32) for _ in range(3)]
        if self.stop == "HT":
            return
        w_in = self.w["diff_w_in"]
        proj_pieces = INNER if kv_only else PIECES
        self.rr2 = 0
        self.rr3 = 0

        def qk_proj(cc, wv, wk, gcol):
            b0 = self.next_set()
            self.pe(self.mm_cols(b0, proj_pieces, wv, cc, lambda k, a, b: hT[:, k, a:b], KC),
                    [wk, "hT"], [("ps", b0), ("ps", b0 + 1), ("ps", b0 + 2)])
            slots = []
            for pi, (a, b) in enumerate(proj_pieces):
                n = b - a
                i6 = self.rr2 % 6
                self.rr2 += 1
                slots.append(i6)
                src = self.bank(b0 + pi)[:, 0:n]
                self.act(sqb[i6][:, 0:n], src, AF.Square, [("ps", b0 + pi)], [("sqb", i6)])
                self.act(qgb[i6][:, 0:n], src, AF.Identity, [("ps", b0 + pi), "dv"], [("qgb", i6)], scale=gcol)
            return b0, slots

        def qk_chain(b0, slots, dst_fn):
            pcs_ = list(enumerate(proj_pieces))
            for pi, (a, b) in pcs_:
                n = b - a
                self.pe([(self.bank(b0 + pi)[:, 0:n], self.ones_b[:], sqb[slots[pi]][:, 0:n], True, True)],
                        [("sqb", slots[pi]), "ones_b"], [("ps", b0 + pi)])
            for pi, (a, b) in pcs_:
                n = b - a
                self.act(rsb[pi][:, 0:n], self.bank(b0 + pi)[:, 0:n], AF.Sqrt, [("ps", b0 + pi)], [("rsb", pi)],
                         scale=1.0 / 128, bias=EPS)
            for pi, (a, b) in pcs_:
                n = b - a
                self.pe([(self.bank(b0 + pi)[:, 0:n], self.perm_b[:], qgb[slots[pi]][:, 0:n], True, True)],
                        [("qgb", slots[pi]), "perm_b"], [("ps", b0 + pi)])
            for pi, (a, b) in pcs_:
                n = b - a
                self.recip(rsb[pi][:, 0:n], rsb[pi][:, 0:n], [("rsb", pi)], [("rsb", pi)])
            for pi, (a, b) in pcs_:
                n = b - a
                self.tt(t2b[pi][:, 0:n], self.bank(b0 + pi)[:, 0:n], rope[:, 1, a:b], ALU.mult,
                        [("ps", b0 + pi), "rope"], [("t2b", pi)])
            for pi, (a, b) in pcs_:
                n = b - a
                self.tt(t1b[pi][:, 0:n], qgb[slots[pi]][:, 0:n], rope[:, 0, a:b], ALU.mult,
                        [("qgb", slots[pi]), "rope"], [("t1b", pi)])
            for pi, (a, b) in pcs_:
                n = b - a
                self.tt(t1b[pi][:, 0:n], t1b[pi][:, 0:n], t2b[pi][:, 0:n], ALU.add, [("t1b", pi), ("t2b", pi)], [("t1b", pi)])
            for pi, (a, b) in pcs_:
                n = b - a
                dst, dk = dst_fn(a, b)
                self.tt(dst, t1b[pi][:, 0:n], rsb[pi][:, 0:n], ALU.mult, [("t1b", pi), ("rsb", pi)], [dk])

        def run_qk(col_base, gcol, store):
            st_ = {}

            def proj(ch):
                if ch % 2 == 0:
                    st_["w"] = self.wload(w_in, KC, col_base + (ch // 2) * 256, 256)
                wv, wk = st_["w"]
                return qk_proj(ch % 2, wv, wk, gcol)

            pend = proj(0)
            for ch in range(16):
                nxt = proj(ch + 1) if ch + 1 < 16 else None
                ks = kst[ch % 2]
                kk = ("kst", ch % 2)
                qk_chain(pend[0], pend[1], lambda a, b, ks=ks, kk=kk: (ks[:, a:b], kk))
                store(ch, ks, kk)
                pend = nxt

        run_qk(2048, dv[:, 1:2], lambda ch, ks, kk: P.dma(
            "sync", "d_k%d" % (ch % 2), kdst[ch * 128:(ch + 1) * 128, :], ks[:, 0:NIN], reads=[kk], writes=[self.okey()]))
        if self.stop == "K":
            return
        for vb in range(8):
            wv, wk = self.wload(w_in, KC, 4096 + vb * 256, 256)
            for ti in range(9):
                bk = self.st_rr % 4
                self.st_rr += 1
                g = self.bank(bk)
                self.pe([(g[:, 0:256], hT[:, k, ti * 128:(ti + 1) * 128], wv[:, k, :], k == 0, k == KC - 1)
                         for k in range(KC)], [wk, "hT"], [("ps", bk)])
                self.act(vst[:, ti, :], g[:, 0:256], AF.Identity, [("ps", bk)], ["vst"])
            P.dma("sync", "d_v", vdst.rearrange("(t p) e -> p t e", p=128)[:, :, vb * 256:(vb + 1) * 256], vst,
                  reads=["vst"], writes=[self.okey()])
        if kv_only or self.stop == "V":
            return
        q_own = self.kv["q_own"]
        sg_own = self.kv["sg_own"]

        run_qk(0, dv[:, 0:1], lambda ch, ks, kk: P.dma(
            "sync", "d_k%d" % (ch % 2), q_own[ch * 128:(ch + 1) * 128, :], ks[:, 0:NT], reads=[kk], writes=[self.okey()]))
        for gb_ in range(8):
            wv, wk = self.wload(w_in, KC, 6144 + gb_ * 256, 256)
            for cc in range(2):
                ch = gb_ * 2 + cc
                ks = kst[ch % 2]
                kk = ("kst", ch % 2)
                b0 = self.next_set()
                self.pe(self.mm_cols(b0, PIECES, wv, cc, lambda k, a, b: hT[:, k, a:b], KC),
                        [wk, "hT"], [("ps", b0), ("ps", b0 + 1), ("ps", b0 + 2)])
                for pi, (a, b) in enumerate(PIECES):
                    self.act(ks[:, a:b], self.bank(b0 + pi)[:, 0:b - a], AF.Silu, [("ps", b0 + pi)], [kk])
                P.dma("sync", "d_k%d" % (ch % 2), sg_own[ch * 128:(ch + 1) * 128, :], ks[:, 0:NT],
                      reads=[kk], writes=[self.okey()])

        if self.stop == "QS":
            return
        P.barrier()
        self.arena_reset()
        NQ = 1040
        ybuf = self.carve([128, KC, NQ], BF16)
        qh = [self.carve([128, 2, NT], BF16) for _ in range(2)]
        sgh = [self.carve([128, 2, NT], BF16) for _ in range(2)]
        kh = [self.carve([128, 2, 2 * NIN], BF16) for _ in range(2)]
        vh = [self.carve([128, 18, 272], BF16) for _ in range(2)]
        pT = [self.carve([128, 2, 256], BF16) for _ in range(4)]
        osb = [self.carve([128, 256], F32) for _ in range(2)]
        o2b = [self.carve([128, 256], F32) for _ in range(2)]
        obf = [self.carve([128, 256], BF16) for _ in range(2)]
        ojk = self.carve([128, 256], BF16)
        sm = self.carve([128, 2, 16], F32)
        zt = self.carve([128, 256], F32)
        P.op("vector", lambda e: e.memset(zt, 0.0), [], ["zt"])
        for i in range(2):
            P.op("vector", lambda e, i=i: e.memset(vh[i][:, :, 256:258], 1.0), [], [("vh1", i)])
        k_oth, v_oth = self.kv["k_oth"], self.kv["v_oth"]
        kd = [kdst, k_oth]
        vd = [vdst, v_oth]
        qblocks = [(0, 256, 0), (256, 512, 256), (512, 768, 512), (768, 1024, 768), (1168, 1184, 1024)]
        scale = 128 ** -0.5
        pti = 0
        epi = 0
        deferred = []
        for h in range(8):
            hb = h % 2
            rows = slice(2 * h * 128, (2 * h + 2) * 128)
            gq = [("qh", hb), ("sgh", hb)]
            gk = [("kh", hb), ("vh", hb)]
            P.dma("sync", "d_qh%d" % hb, qh[hb], q_own[rows, :].rearrange("(c d) t -> d c t", d=128),
                  reads=[], writes=gq)
            P.dma("sync", "d_qh%d" % hb, sgh[hb], sg_own[rows, :].rearrange("(c d) t -> d c t", d=128),
                  reads=[], writes=gq)
            for r in range(2):
                P.dma("sync", "d_kh%d" % hb, kh[hb][:, :, r * NIN:(r + 1) * NIN],
                      kd[r][rows, :].rearrange("(c d) t -> d c t", d=128), reads=[], writes=gk)
                P.dma("sync", "d_kh%d" % hb, vh[hb][:, r * 9:(r + 1) * 9, 0:256],
                      vd[r].rearrange("(t p) e -> p t e", p=128)[:, :, h * 256:(h + 1) * 256],
                      reads=[("vh1", hb)], writes=gk)
            for (q0, q1, y0) in qblocks:
                nq = q1 - q0
                nsub = (nq + 127) // 128
                accb = [[0 + 2 * s_ + i for i in range(2)] for s_ in range(nsub)]
                SB = (4, 5, 7)

                def s_issue(kt, q0=q0, q1=q1, nq=nq, hb=hb):
                    sbk = SB[kt % 3]
                    S = self.bank(sbk)
                    self.pe([(S[:, i * 256:i * 256 + nq], kh[hb][:, i, kt * 128:(kt + 1) * 128], qh[hb][:, i, q0:q1], True, True)
                             for i in range(2)], [("kh", hb), ("qh", hb)], [("ps", sbk)])

                s_issue(0)
                s_issue(1)
                for kt in range(18):
                    if kt == 3:
                        for f_ in deferred:
                            f_()
                        deferred.clear()
                    if kt + 2 < 18:
                        s_issue(kt + 2)
                    sbk = SB[kt % 3]
                    S = self.bank(sbk)
                    pt = pT[pti % 4]
                    pk_ = ("pT", pti % 4)
                    pti += 1
                    self.act(pt[:, :, 0:nq], S[:, 0:512].rearrange("p (i q) -> p i q", q=256)[:, :, 0:nq], AF.Exp,
                             [("ps", sbk)], [pk_], scale=scale)
                    mms = []
                    for s_ in range(nsub):
                        m = min(128, nq - s_ * 128)
                        for i in range(2):
                            mms.append((self.bank(accb[s_][i])[0:m, 0:258], pt[:, i, s_ * 128:s_ * 128 + m],
                                        vh[hb][:, kt, 0:258], kt == 0, kt == 17))
                    self.pe(mms, [pk_, ("vh", hb)], [("ps", accb[s_][i]) for s_ in range(nsub) for i in range(2)])
                eps_ = []
                for s_ in range(nsub):
                    m = min(128, nq - s_ * 128)
                    e2 = epi % 2
                    epi += 1
                    eps_.append((s_, m, e2))
                    O0 = self.bank(accb[s_][0])
                    O1 = self.bank(accb[s_][1])
                    sk = ("sm", e2)
                    smv = sm[:, e2, :]
                    self.recip(smv[0:m, 0:1], O0[0:m, 256:257], [("ps", accb[s_][0])], [sk])
                    self.recip(smv[0:m, 1:2], O1[0:m, 256:257], [("ps", accb[s_][1])], [sk])
                    self.tt(smv[0:m, 2:3], smv[0:m, 1:2], neglam[0:m, :], ALU.mult, [sk, "dsm"], [sk])
                    self.stt(o2b[e2][0:m, :], O1[0:m, 0:256], smv[0:m, 2:3], zt[0:m, :], ALU.mult, ALU.add,
                             [("ps", accb[s_][1]), sk, "zt"], [("o2b", e2)])
                    self.stt(osb[e2][0:m, :], O0[0:m, 0:256], smv[0:m, 0:1], o2b[e2][0:m, :], ALU.mult, ALU.add,
                             [("ps", accb[s_][0]), sk, ("o2b", e2)], [("osb", e2)])
                def ep2(eps_=eps_, h=h, hb=hb, q0=q0, y0=y0):
                  for (s_, m, e2) in eps_:
                      sk = ("sm", e2)
                      smv = sm[:, e2, :]
                      self.act(ojk[0:m, :], osb[e2][0:m, :], AF.Square, [("osb", e2)], ["ojk", sk], accum_out=smv[0:m, 3:4])
                      self.act(smv[0:m, 4:5], smv[0:m, 3:4], AF.Sqrt, [sk], [sk], scale=1.0 / 256, bias=EPS)
                      self.recip(smv[0:m, 6:7], smv[0:m, 4:5], [sk], [sk])
                      if m < 128:
                          P.op("vector", lambda e, e2=e2: e.memset(obf[e2][:, :], 0.0), [], [("obf", e2)])
                      self.ts(obf[e2][0:m, :], osb[e2][0:m, :], smv[0:m, 6:7], None, ALU.mult, None,
                              [("osb", e2), sk], [("obf", e2)])
                      tb = 6
                      T = self.bank(tb, BF16)[:, e2 * 256:(e2 + 1) * 256]
                      self.pet([(T[:, ec * 128:(ec + 1) * 128], obf[e2][:, ec * 128:(ec + 1) * 128], self.ident_b[:])
                                for ec in range(2)], [("obf", e2), "ident_b"], [("ps", tb)])
                      c0 = q0 + s_ * 128
                      yc = y0 + s_ * 128
                      for ec in range(2):
                          self.stt(ybuf[:, 2 * h + ec, yc:yc + m], T[:, ec * 128:ec * 128 + m], dsm[:, 6 + ec:7 + ec],
                                   sgh[hb][:, ec, c0:c0 + m], ALU.mult, ALU.mult,
                                   [("ps", tb), "dsm", ("sgh", hb)], [("y", 2 * h + ec)])

                deferred.append(ep2)
        for f_ in deferred:
            f_()
        deferred.clear()
        if self.stop == "ATT":
            return
        w_out = self.w["diff_w_out"]
        ypieces = [(0, 512), (512, 1024), (1024, 1040)]
        xcols = [(0, 512, 0), (512, 1024, 0), (1168, 1184, 0)]
        for nb in range(8):
            wv, wk = self.wload(w_out, KC, nb * 256, 256)
            for cc in range(2):
                mc = nb * 2 + cc
                b0 = self.next_set()
                self.pe(self.mm_cols(b0, ypieces, wv, cc, lambda k, a, b: ybuf[:, k, a:b], KC),
                        [wk] + [("y", k) for k in range(KC)], [("ps", b0), ("ps", b0 + 1), ("ps", b0 + 2)])
                self.residual(l, mc, b0, xcols)


def _cols(x_b, ctx_b, s):
    L, C = x_b.shape[0], ctx_b.shape[0]
    hx, hc = L // 2, C // 2
    rows = np.zeros((NT, D), np.float32)
    rows[0:1024] = x_b[s * hx:(s + 1) * hx]
    rows[1024:1152] = ctx_b[s * hc:(s + 1) * hc]

    def halo(src, lo, n, dst):
        a, b = max(lo, 0), min(lo + n, src.shape[0])
        if b > a:
            rows[dst + (a - lo):dst + (b - lo)] = src[a:b]

    halo(ctx_b, s * hc - 8, 8, 1152)
    halo(ctx_b, (s + 1) * hc, 8, 1160)
    halo(x_b, s * hx - 8, 8, 1168)
    halo(x_b, (s + 1) * hx, 8, 1176)
    return np.ascontiguousarray(rows.reshape(NT, KC, 128).transpose(2, 1, 0))


def _masks(s):
    hm = np.zeros((128, 32), np.float32)
    hm[:, 0:8] = 1.0 if s == 1 else 0.0
    hm[:, 8:16] = 1.0 if s == 0 else 0.0
    hm[:, 16:24] = 1.0 if s == 1 else 0.0
    hm[:, 24:32] = 1.0 if s == 0 else 0.0
    ec = np.ones((4, 4, 8), np.float32)
    for gi, w in enumerate((2, 4, 8, 16)):
        for seg, (L, half) in enumerate(((2048, 1024), (256, 128))):
            for side in range(2):
                for t8 in range(8):
                    t = s * half + (t8 if side == 0 else half - 8 + t8)
                    lo = max(t - w // 2, 0)
                    hi = min(t + w - w // 2, L)
                    ec[gi, seg * 2 + side, t8] = w / float(hi - lo)
    ecb = np.broadcast_to(ec.reshape(1, 128), (128, 128)).astype(np.float32).copy()
    return hm, ecb


def _rope(s):
    pos = np.zeros(NT, np.int64)
    valid = np.zeros(NT, bool)
    pos[0:1024] = s * 1024 + np.arange(1024)
    valid[0:1024] = True
    pos[1168:1176] = s * 1024 - 8 + np.arange(8)
    pos[1176:1184] = (s + 1) * 1024 + np.arange(8)
    valid[1168:1184] = True
    pos = np.clip(pos, 0, 2047)
    row = (pos // 64).astype(np.float32)
    col = (pos % 64).astype(np.float32)
    inv = (np.float32(10000.0) ** (-np.arange(0, 64, 2, dtype=np.float32) / np.float32(64))).astype(np.float32)
    d = np.arange(128)
    ang = np.where((d < 64)[:, None], row[None, :] * inv[d % 32][:, None], col[None, :] * inv[d % 32][:, None]).astype(np.float32)
    cos = np.cos(ang).astype(np.float32)
    sin = np.sin(ang).astype(np.float32)
    cos[:, ~valid] = 1.0
    sin[:, ~valid] = 0.0
    return np.ascontiguousarray(np.stack([cos, sin], axis=1))


def _perm():
    Pm = np.zeros((128, 128), np.float32)
    for base in (0, 64):
        for i in range(32):
            Pm[base + 32 + i, base + i] = -1.0
            Pm[base + i, base + 32 + i] = 1.0
    return Pm


def _fm(v, nchunk):
    return np.ascontiguousarray(np.asarray(v, np.float32).reshape(nchunk, 128).T)


_NC_CACHE = {}


def _get_nc(mode, dump=False):
    key = (mode, dump)
    if key not in _NC_CACHE:
        _NC_CACHE[key] = Builder(mode, dump).build()
    return _NC_CACHE[key]


def _common_inputs(inp):
    f = lambda a: np.ascontiguousarray(np.asarray(a, np.float32))
    norm_gT = np.stack([_fm(inp["norm_g"][l], 16) for l in range(4)], axis=1)
    ada_bT = np.stack([_fm(inp["ada_b"][l], 48) for l in range(4)], axis=1)
    ada_bT = np.ascontiguousarray(np.repeat(ada_bT[:, :, :, None], 2, axis=3))
    pool_scaleT = np.stack([_fm(inp["pool_scale"][j], 32) for j in range(2)], axis=1)
    ln_gbT = np.stack([_fm(inp["gmlp_ln_g"][0], 32), _fm(inp["gmlp_ln_b"][0], 32)], axis=1)
    w_sT = np.ascontiguousarray(np.asarray(inp["gmlp_w_s"][0], np.float32).transpose(2, 0, 1))
    b_s_row = f(inp["gmlp_b_s"][0]).reshape(1, 1024)
    sub = np.asarray(inp["diff_subln_g"][0], np.float32)
    dvec = np.stack([inp["diff_q_norm_g"][0], inp["diff_k_norm_g"][0], inp["diff_lq1"][0], inp["diff_lk1"][0],
                     inp["diff_lq2"][0], inp["diff_lk2"][0], sub[0:128], sub[128:256]], axis=1).astype(np.float32)
    return dict(
        ident=np.eye(128, dtype=np.float32), permT=_perm(),
        norm_gT=np.ascontiguousarray(norm_gT), ada_bT=ada_bT, ada_w=f(inp["ada_w"]),
        pool_w_in=f(inp["pool_w_in"]), pool_w_grp=f(inp["pool_w_grp"]), pool_scaleT=np.ascontiguousarray(pool_scaleT),
        pool_w_out=f(inp["pool_w_out"]), gmlp_w_in=f(inp["gmlp_w_in"][0]), ln_gbT=np.ascontiguousarray(ln_gbT),
        w_sT=w_sT, b_s_row=b_s_row, gmlp_w_out=f(inp["gmlp_w_out"][0]), diff_w_in=f(inp["diff_w_in"][0]),
        dvec=np.ascontiguousarray(dvec), diff_w_out=f(inp["diff_w_out"][0]),
    )


def _core_inputs(inp, core, which):
    b, s = core // 2, core % 2
    x_b = np.asarray(inp["x"][b], np.float32)
    ctx_b = np.asarray(inp["ctx"][b], np.float32)
    d = {}
    for tag, ss in which:
        hm, ec = _masks(ss)
        d["x" + tag] = _cols(x_b, ctx_b, ss)
        d["hmask_" + tag] = hm
        d["ecorr_" + tag] = ec
        d["rope_" + tag] = _rope(ss)
    cv = np.stack([_fm(inp["c"][b], 16), _fm(inp["c_ctx"], 16)], axis=2)
    d["cvec"] = np.ascontiguousarray(cv)
    return d


def _assemble(results):
    out = np.zeros((4, 2048, D), np.float32)
    for core in range(8):
        b, s = core // 2, core % 2
        o = results[core]["out"]
        out[b, s * 1024:(s + 1) * 1024, :] = o.transpose(2, 1, 0).reshape(1024, D)
    return out


MODE = "fused"


def kernel(**inp):
    inp = {k: np.asarray(v) for k, v in inp.items()}
    common = _common_inputs(inp)
    if MODE == "fused":
        nc = _get_nc("fused")
        in_maps = []
        for core in range(8):
            s = core % 2
            d = dict(common)
            d.update(_core_inputs(inp, core, [("a", 1 - s), ("b", s)]))
            in_maps.append(d)
        res = run_bass_kernel_spmd(nc, in_maps, core_ids=list(range(8)))
        return _assemble(res.results)
    raise NotImplementedError
```
